# Optimizing a Trainium2 kernel written in Bass

```python
import math
import jax, jax.numpy as jnp
from jax import lax
import numpy as np

D_MODEL = 1024
BATCH = 4
SEQ = 4096
DEPTH = 2

N_EVEN = (DEPTH + 1) // 2
N_ODD = DEPTH // 2

POOL_WINDOWS = (2, 4, 8, 16)
N_POOL_GROUPS = len(POOL_WINDOWS)
POOL_DIM = D_MODEL // 2
POOL_GROUP = POOL_DIM // N_POOL_GROUPS

HEAD_DIM = 64
N_HEADS = (D_MODEL // 2) // HEAD_DIM
N_KV_HEADS = 2
GQ = N_HEADS // N_KV_HEADS
Q_DIM = N_HEADS * HEAD_DIM
KV_DIM = N_KV_HEADS * HEAD_DIM
WINDOW = 128
BLOCK = 128
ROPE_THETA = 10000.0
MIX_IN_DIM = POOL_DIM + Q_DIM + 2 * KV_DIM
MIX_OUT_DIM = POOL_DIM + Q_DIM
MAX_POS_OFFSET = 1024

SSM_EXPAND = 2
SSM_D_INNER = SSM_EXPAND * D_MODEL
SSM_HEAD_DIM = 64
SSM_HEADS = SSM_D_INNER // SSM_HEAD_DIM
SSM_GROUPS = 8
SSM_STATE = 128
SSM_CONV = 4
SSM_CHUNK = 128
SSM_CONV_DIM = SSM_D_INNER + 2 * SSM_GROUPS * SSM_STATE
SSM_IN_DIM = SSM_D_INNER + SSM_CONV_DIM + SSM_HEADS

D_FF = 2816
FFN_CONV = 3

NORM_EPS = 1e-6
SSM_NORM_EPS = 1e-5

kernel_name = "hybrid_pool_swa_ssd_convffn"


def rms_norm(x, w, eps=NORM_EPS):
    xf = x.astype(jnp.float32)
    y = xf * lax.rsqrt(jnp.mean(xf * xf, axis=-1, keepdims=True) + eps)
    return (y * w.astype(jnp.float32)).astype(x.dtype)


def causal_dwconv(x, w, b):
    K, C = w.shape
    y = lax.conv_general_dilated(
        x, w.astype(x.dtype)[:, None, :], window_strides=(1,), padding=[(K - 1, 0)],
        dimension_numbers=("NWC", "WIO", "NWC"), feature_group_count=C)
    return y + b.astype(x.dtype)


def rope_tables(positions):
    inv_freq = ROPE_THETA ** (-jnp.arange(0, HEAD_DIM, 2, dtype=jnp.float32) / HEAD_DIM)
    ang = positions.astype(jnp.float32)[..., None] * inv_freq
    ang = jnp.concatenate([ang, ang], axis=-1)
    return jnp.cos(ang)[:, :, None, :], jnp.sin(ang)[:, :, None, :]


def apply_rope(t, cos, sin):
    tf = t.astype(jnp.float32)
    half = HEAD_DIM // 2
    rot = jnp.concatenate([-tf[..., half:], tf[..., :half]], axis=-1)
    return (tf * cos + rot * sin).astype(t.dtype)


def multiscale_pool(u):
    B, S, _ = u.shape
    ug = u.reshape(B, S, N_POOL_GROUPS, POOL_GROUP).astype(jnp.float32)
    cs = jnp.pad(jnp.cumsum(ug, axis=1), ((0, 0), (1, 0), (0, 0), (0, 0)))
    t1 = jnp.arange(1, S + 1)
    means = []
    for g, w in enumerate(POOL_WINDOWS):
        upper = cs[:, 1:, g]
        lower = jnp.pad(cs[:, :S + 1 - w, g], ((0, 0), (w - 1, 0), (0, 0)))
        cnt = jnp.minimum(t1, w).astype(jnp.float32)[None, :, None]
        means.append((upper - lower) / cnt)
    return jnp.stack(means, axis=2) - ug


def sliding_window_attention(q, k, v, sinks):
    B, S, _, _ = q.shape
    nb = S // BLOCK
    qb = q.reshape(B, nb, BLOCK, N_KV_HEADS, GQ, HEAD_DIM)

    def with_prev(t):
        t = t.reshape(B, nb, BLOCK, N_KV_HEADS, HEAD_DIM)
        prev = jnp.pad(t[:, :-1], ((0, 0), (1, 0), (0, 0), (0, 0), (0, 0)))
        return jnp.concatenate([prev, t], axis=2)

    kk, vv = with_prev(k), with_prev(v)
    s = jnp.einsum("bnqkgd,bnskd->bkgnqs", qb, kk).astype(jnp.float32) * (HEAD_DIM ** -0.5)
    qi = jnp.arange(BLOCK)[:, None]
    kj = jnp.arange(2 * BLOCK)[None, :]
    rel = qi + BLOCK - kj
    band = (rel >= 0) & (rel < WINDOW)
    valid = (jnp.arange(nb)[:, None, None] > 0) | (kj[None] >= BLOCK)
    mask = band[None] & valid
    s = jnp.where(mask, s, -jnp.inf)
    sink = jnp.broadcast_to(sinks.astype(jnp.float32).reshape(1, N_KV_HEADS, GQ, 1, 1, 1),
                            s.shape[:-1] + (1,))
    p = jax.nn.softmax(jnp.concatenate([s, sink], axis=-1), axis=-1)[..., :-1]
    o = jnp.einsum("bkgnqs,bnskd->bnqkgd", p.astype(v.dtype), vv)
    return o.reshape(B, S, Q_DIM)


def pool_attention_mixer(h, cos, sin, w_in, pool_w, pool_scale, sinks, w_out):
    B, S, _ = h.shape
    proj = h @ w_in
    u, q, k, v = jnp.split(proj, [POOL_DIM, POOL_DIM + Q_DIM, POOL_DIM + Q_DIM + KV_DIM], axis=-1)
    pooled = multiscale_pool(u).astype(h.dtype)
    pooled = jnp.einsum("bsgc,gcd->bsgd", pooled, pool_w).reshape(B, S, POOL_DIM) * pool_scale
    q = apply_rope(q.reshape(B, S, N_HEADS, HEAD_DIM), cos, sin)
    k = apply_rope(k.reshape(B, S, N_KV_HEADS, HEAD_DIM), cos, sin)
    v = v.reshape(B, S, N_KV_HEADS, HEAD_DIM)
    attn = sliding_window_attention(q, k, v, sinks)
    return jnp.concatenate([pooled, attn], axis=-1) @ w_out


def ssd_scan(x, dt, A, Bm, Cm):
    b, s, h, p = x.shape
    g, n = Bm.shape[2:]
    r = h // g
    c = s // SSM_CHUNK
    X = (x * dt[..., None]).reshape(b, c, SSM_CHUNK, g, r, p)
    a = (dt * A).reshape(b, c, SSM_CHUNK, g, r).transpose(0, 3, 4, 1, 2)
    a_cs = jnp.cumsum(a, axis=-1)
    Bc = Bm.reshape(b, c, SSM_CHUNK, g, n)
    Cc = Cm.reshape(b, c, SSM_CHUNK, g, n)
    tril = jnp.tril(jnp.ones((SSM_CHUNK, SSM_CHUNK), dtype=bool))
    seg = a_cs[..., :, None] - a_cs[..., None, :]
    Lmat = jnp.exp(jnp.where(tril, seg, -jnp.inf))
    CB = jnp.einsum("bclgn,bcsgn->bgcls", Cc, Bc)
    y_diag = jnp.einsum("bgcls,bgrcls,bcsgrp->bclgrp", CB, Lmat, X)
    decay = jnp.exp(a_cs[..., -1:] - a_cs)
    states = jnp.einsum("bclgn,bgrcl,bclgrp->bcgrpn", Bc, decay, X)
    chunk_decay = jnp.exp(a_cs[..., -1])

    def step(state, inp):
        st, dec = inp
        return state * dec[..., None, None] + st, state

    h0 = jnp.zeros((b, g, r, p, n), jnp.float32)
    _, prev = lax.scan(step, h0, (jnp.moveaxis(states, 1, 0), jnp.moveaxis(chunk_decay, 3, 0)))
    y_off = jnp.einsum("bclgn,cbgrpn,bgrcl->bclgrp", Cc, prev, jnp.exp(a_cs))
    return (y_diag + y_off).reshape(b, s, h, p)


def ssd_mixer(h, w_in, conv_w, conv_b, dt_bias, A_log, D_skip, norm_w, w_out):
    B, S, _ = h.shape
    proj = h @ w_in
    z, xbc, dt = jnp.split(proj, [SSM_D_INNER, SSM_D_INNER + SSM_CONV_DIM], axis=-1)
    xbc = jax.nn.silu(causal_dwconv(xbc, conv_w, conv_b))
    xs, Bm, Cm = jnp.split(xbc, [SSM_D_INNER, SSM_D_INNER + SSM_GROUPS * SSM_STATE], axis=-1)
    xs = xs.reshape(B, S, SSM_HEADS, SSM_HEAD_DIM).astype(jnp.float32)
    Bm = Bm.reshape(B, S, SSM_GROUPS, SSM_STATE).astype(jnp.float32)
    Cm = Cm.reshape(B, S, SSM_GROUPS, SSM_STATE).astype(jnp.float32)
    dt = jax.nn.softplus(dt.astype(jnp.float32) + dt_bias.astype(jnp.float32))
    A = -jnp.exp(A_log.astype(jnp.float32))
    y = ssd_scan(xs, dt, A, Bm, Cm) + D_skip.astype(jnp.float32)[:, None] * xs
    y = y.reshape(B, S, SSM_D_INNER) * jax.nn.silu(z.astype(jnp.float32))
    y = rms_norm(y, norm_w, SSM_NORM_EPS).astype(h.dtype)
    return y @ w_out


def conv_ffn(h, w_up, conv_w, conv_b, w_down):
    hid = causal_dwconv(h @ w_up, conv_w, conv_b)
    u, g = jnp.split(hid, 2, axis=-1)
    return (jax.nn.silu(g) * u) @ w_down


def setup_inputs(seed: int = 0) -> dict:
    key = jax.random.key(seed)
    ks = jax.random.split(key, 24)
    f32 = jnp.float32
    nrm = lambda k, shape, scale: jax.random.normal(k, shape, f32) * scale
    x = jax.random.normal(ks[0], (BATCH, SEQ, D_MODEL), f32)
    offsets = jax.random.randint(ks[1], (BATCH, 1), 0, MAX_POS_OFFSET, dtype=jnp.int32)
    positions = (offsets + jnp.arange(SEQ, dtype=jnp.int32)[None, :]).astype(jnp.int32)
    dt0 = jnp.exp(jax.random.uniform(ks[13], (N_ODD, SSM_HEADS), f32, math.log(1e-3), math.log(1e-1)))
    return {
        "x": x,
        "positions": positions,
        "norm_mix": 1.0 + nrm(ks[2], (DEPTH, D_MODEL), 0.02),
        "norm_ffn": 1.0 + nrm(ks[3], (DEPTH, D_MODEL), 0.02),
        "norm_final": 1.0 + nrm(ks[4], (D_MODEL,), 0.02),
        "mix_w_in": nrm(ks[5], (N_EVEN, D_MODEL, MIX_IN_DIM), D_MODEL ** -0.5),
        "pool_w": nrm(ks[6], (N_EVEN, N_POOL_GROUPS, POOL_GROUP, POOL_GROUP), POOL_GROUP ** -0.5),
        "pool_scale": 1.0 + nrm(ks[7], (N_EVEN, POOL_DIM), 0.1),
        "attn_sinks": nrm(ks[8], (N_EVEN, N_HEADS), 1.0),
        "mix_w_out": nrm(ks[9], (N_EVEN, MIX_OUT_DIM, D_MODEL), MIX_OUT_DIM ** -0.5),
        "ssm_w_in": nrm(ks[10], (N_ODD, D_MODEL, SSM_IN_DIM), D_MODEL ** -0.5),
        "ssm_conv_w": nrm(ks[11], (N_ODD, SSM_CONV, SSM_CONV_DIM), SSM_CONV ** -0.5),
        "ssm_conv_b": nrm(ks[12], (N_ODD, SSM_CONV_DIM), 0.02),
        "ssm_dt_bias": dt0 + jnp.log(-jnp.expm1(-dt0)),
        "ssm_A_log": jnp.log(jax.random.uniform(ks[14], (N_ODD, SSM_HEADS), f32, 1.0, 16.0)),
        "ssm_D": 1.0 + nrm(ks[15], (N_ODD, SSM_HEADS), 0.1),
        "ssm_norm": 1.0 + nrm(ks[16], (N_ODD, SSM_D_INNER), 0.02),
        "ssm_w_out": nrm(ks[17], (N_ODD, SSM_D_INNER, D_MODEL), SSM_D_INNER ** -0.5),
        "ffn_w_up": nrm(ks[18], (DEPTH, D_MODEL, 2 * D_FF), D_MODEL ** -0.5),
        "ffn_conv_w": nrm(ks[19], (DEPTH, FFN_CONV, 2 * D_FF), FFN_CONV ** -0.5),
        "ffn_conv_b": nrm(ks[20], (DEPTH, 2 * D_FF), 0.02),
        "ffn_w_down": nrm(ks[21], (DEPTH, D_FF, D_MODEL), D_FF ** -0.5),
    }


def reference(x, positions, norm_mix, norm_ffn, norm_final, mix_w_in, pool_w, pool_scale,
              attn_sinks, mix_w_out, ssm_w_in, ssm_conv_w, ssm_conv_b, ssm_dt_bias, ssm_A_log,
              ssm_D, ssm_norm, ssm_w_out, ffn_w_up, ffn_conv_w, ffn_conv_b, ffn_w_down):
    cos, sin = rope_tables(positions)
    for i in range(DEPTH):
        j = i // 2
        h = rms_norm(x, norm_mix[i])
        if i % 2 == 0:
            x = x + pool_attention_mixer(h, cos, sin, mix_w_in[j], pool_w[j], pool_scale[j],
                                         attn_sinks[j], mix_w_out[j])
        else:
            x = x + ssd_mixer(h, ssm_w_in[j], ssm_conv_w[j], ssm_conv_b[j], ssm_dt_bias[j],
                              ssm_A_log[j], ssm_D[j], ssm_norm[j], ssm_w_out[j])
        x = x + conv_ffn(rms_norm(x, norm_ffn[i]), ffn_w_up[i], ffn_conv_w[i], ffn_conv_b[i],
                         ffn_w_down[i])
    return rms_norm(x, norm_final)
```

```python
import numpy as np
import concourse.bass as bass
import concourse.mybir as mybir
from concourse.bass_utils import run_bass_kernel_spmd
from contextlib import ExitStack

F32 = mybir.dt.float32
BF16 = mybir.dt.bfloat16
I32 = mybir.dt.int32
ALU = mybir.AluOpType
AF = mybir.ActivationFunctionType

D = 1024
SEQ = 4096
TT = 512
DFF = 2816
NEG = -30000.0

C_NM0, C_NF0, C_NM1, C_NF1, C_NFIN, C_PSC = 0, 8, 16, 24, 32, 40
C_FCW = 44
C_FCB = 308
C_SCW = 396
C_SCB = 524
C_INVF, C_SGN, C_SINK = 556, 557, 558
C_ICNT = 562
C_EPS, C_EPS2 = 626, 627
NCV = 628
R_NW, R_D, R_ALOG, R_DTB = 0, 2048, 2080, 2112
NR = 2144


class Buf:
    __slots__ = ("name", "lw", "rd")

    def __init__(self, name):
        self.name = name
        self.lw = None
        self.rd = []


class Prog:
    ENGS = ("pe", "act", "dve", "pool", "sp")

    def __init__(self, nc, stack, n_dma_sems=36):
        self.nc = nc
        self.streams = {e: [] for e in self.ENGS}
        self.cnt = {e: 0 for e in self.ENGS}
        self.sem = {e: stack.enter_context(nc.semaphore("s_" + e)) for e in self.ENGS}
        self.seen = {e: {} for e in self.ENGS}
        self.dsem = [stack.enter_context(nc.semaphore("d%d" % i)) for i in range(n_dma_sems)]
        self.dcnt = [0] * n_dma_sems
        self.dlast = [None] * n_dma_sems
        self.snaps = {}

    def _need(self, eng, tokens):
        for tok in tokens:
            if tok is None:
                continue
            sem, val, src = tok
            if src == "pe" and eng == "pe":
                continue
            key = id(sem)
            if self.seen[eng].get(key, 0) >= val:
                continue
            self.seen[eng][key] = val
            self.streams[eng].append(("wait", sem, val))
            snap = self.snaps.get((key, val))
            if snap:
                mine = self.seen[eng]
                for k2, v2 in snap.items():
                    if mine.get(k2, 0) < v2:
                        mine[k2] = v2

    @staticmethod
    def _compact(toks):
        best = {}
        for t in toks:
            if t is None:
                continue
            k = id(t[0])
            if k not in best or best[k][1] < t[1]:
                best[k] = t
        return list(best.values())

    def _deps(self, reads, writes):
        toks = []
        for b in reads:
            toks.append(b.lw)
        for b in writes:
            toks.append(b.lw)
            toks.extend(b.rd)
        return toks

    def _commit(self, tok, reads, writes):
        for b in reads:
            b.rd.append(tok)
            if len(b.rd) > 12:
                b.rd = self._compact(b.rd)
        for b in writes:
            b.lw = tok
            b.rd = []

    def op(self, eng, fn, reads=(), writes=()):
        self._need(eng, self._deps(reads, writes))
        self.cnt[eng] += 1
        tok = (self.sem[eng], self.cnt[eng], eng)
        self.snaps[(id(self.sem[eng]), self.cnt[eng])] = dict(self.seen[eng])
        self.streams[eng].append(("op", fn, self.sem[eng], 1))
        self._commit(tok, reads, writes)
        return tok

    def dma(self, q, out, in_, reads=(), writes=(), slot=0):
        self._need(q, self._deps(reads, writes) + [self.dlast[slot]])
        self.dcnt[slot] += 16
        sem = self.dsem[slot]
        tok = (sem, self.dcnt[slot], "dma")
        self.snaps[(id(sem), self.dcnt[slot])] = dict(self.seen[q])
        self.streams[q].append(("op", (lambda e: e.dma_start(out=out, in_=in_)), sem, 16))
        self.dlast[slot] = tok
        self._commit(tok, reads, writes)
        return tok

    def wait_all(self, eng, toks):
        self._need(eng, toks)

    def alias(self, new_bufs, old_bufs):
        toks = []
        for b in old_bufs:
            toks.append(b.lw)
            toks.extend(b.rd)
        toks = self._compact(toks)
        for nb in new_bufs:
            nb.lw = None
            nb.rd = list(toks)

    def emit(self):
        nc = self.nc
        with nc.Block() as block:
            def run(e):
                def body(eng):
                    for item in self.streams[e]:
                        if item[0] == "wait":
                            eng.wait_ge(item[1], item[2])
                        else:
                            item[1](eng).then_inc(item[2], item[3])
                return body
            block.tensor(run("pe"))
            block.scalar(run("act"))
            block.vector(run("dve"))
            block.gpsimd(run("pool"))
            block.sync(run("sp"))


class Builder:
    def __init__(self, S, NT, dbg_stage=99):
        self.S = S
        self.NT = NT
        self.dbg_stage = dbg_stage
        self.nc = bass.Bass("TRN2", target_bir_lowering=False)

    def mm(self, out, lhsT, rhs, start, stop, reads, writes):
        self.P.op("pe", lambda e: e.matmul(out, lhsT=lhsT, rhs=rhs, start=start, stop=stop), reads, writes)

    def tr(self, out, in_, ident, reads, writes):
        self.P.op("pe", lambda e: e.transpose(out, in_, ident), reads, writes)

    def act(self, out, in_, func, reads, writes, **kw):
        self.P.op("act", lambda e: e.activation(out=out, in_=in_, func=func, **kw), reads, writes)

    def tt(self, eng, out, in0, in1, op, reads, writes):
        self.P.op(eng, lambda e: e.tensor_tensor(out=out, in0=in0, in1=in1, op=op), reads, writes)

    def ts(self, eng, out, in0, s1, s2, op0, op1, reads, writes):
        if s2 is None:
            self.P.op(eng, lambda e: e.tensor_scalar(out=out, in0=in0, scalar1=s1, scalar2=None, op0=op0), reads, writes)
        else:
            self.P.op(eng, lambda e: e.tensor_scalar(out=out, in0=in0, scalar1=s1, scalar2=s2, op0=op0, op1=op1), reads, writes)

    def stt(self, out, in0, scalar, in1, op0, op1, reads, writes):
        self.P.op("dve", lambda e: e.scalar_tensor_tensor(out=out, in0=in0, scalar=scalar, in1=in1, op0=op0, op1=op1), reads, writes)

    def cp(self, eng, out, in_, reads, writes):
        if eng == "act":
            self.act(out, in_, AF.Copy, reads, writes)
        else:
            self.P.op(eng, lambda e: e.tensor_copy(out=out, in_=in_), reads, writes)

    def recip(self, out, in_, reads, writes):
        self.P.op("dve", lambda e: e.reciprocal(out=out, in_=in_), reads, writes)

    def bank(self):
        i = self.bank_i
        self.bank_i = (i + 1) % len(self.banks)
        return self.banks[i]

    def tbank(self):
        i = self.tbank_i
        self.tbank_i = (i + 1) % len(self.tbanks)
        return self.tbanks[i]

    def scr(self):
        i = self.scr_i
        self.scr_i = (i + 1) % len(self.scrs)
        return self.scrs[i]

    def carve(self, nbytes):
        off = self.arena_off
        assert off % 4 == 0
        self.arena_off += (nbytes + 3) // 4 * 4
        assert self.arena_off <= self.ARENA_BYTES, (self.arena_off, self.ARENA_BYTES)
        return off

    def a_bf(self, off, n):
        return self.arena[:, off // 4: off // 4 + n // 2].bitcast(BF16)

    def a_f32(self, off, n):
        return self.arena[:, off // 4: off // 4 + n]

    def stage_bufs(self, names):
        bufs = {n: Buf(n) for n in names}
        self.P.alias(list(bufs.values()), self.arena_users)
        self.arena_users = list(bufs.values())
        self.arena_off = 0
        return bufs

    def wget(self, nk, ncol):
        blk = self.wsched[self.w_cur]
        assert blk[1] == nk and blk[2] == ncol, (self.w_cur, blk[1:], nk, ncol)
        hi = min(len(self.wsched), self.w_cur + self.NSLOT)
        NB = self.nblk_tile
        while self.w_issued < hi:
            i = self.w_issued
            ap_d, k_, c_ = self.wsched[i]
            s = i % self.NSLOT
            n = k_ * c_
            bi = i % NB
            scr_ap = self.wscr[bi * 128:(bi + 1) * 128, 0:n]
            if i < NB or not self.USE_WSCR:
                self.P.dma("pool", self.wslot[s][:, 0:n], ap_d, writes=[self.Bw[s]], slot=s)
                if self.USE_WSCR and self.NT > 1:
                    self.P.dma("sp", scr_ap, self.wslot[s][:, 0:n], reads=[self.Bw[s]], writes=[self.Bwscr[bi]], slot=24 + (i % 4))
            else:
                self.P.dma("sp", self.wslot[s][:, 0:n], scr_ap, reads=[self.Bwscr[bi]], writes=[self.Bw[s]], slot=s)
            self.w_issued += 1
        s = self.w_cur % self.NSLOT
        self.w_cur += 1
        return self.wslot[s][:, 0:nk * ncol].rearrange("p (k c) -> p k c", k=nk), self.Bw[s]

    def make_wsched(self):
        def blks(w, nb, nk, ncol):
            return [(w[b * 128:(b + 1) * 128, :], nk, ncol) for b in range(nb)]
        one = blks(self.win0a, 3, 8, 512) + blks(self.win0b, 1, 8, 384) + blks(self.wout0, 2, 8, 512)

        def ffn(l):
            return blks(self.fup[l], 11, 8, 512) + blks(self.fdn[l], 8, 22, 128)
        if self.dbg_stage < 1:
            one = []
        if self.dbg_stage >= 2:
            one += ffn(0)
        sa = blks(self.sina, 12, 8, 512)
        ssd = []
        for b in range(8):
            ssd.append(sa[4 + b])
            if b % 2 == 1:
                ssd.append(sa[b // 2])
        ssd += blks(self.sinb, 1, 8, 32) + blks(self.sout, 8, 16, 128)
        if self.dbg_stage >= 3:
            one += ssd
        if self.dbg_stage >= 4:
            one += ffn(1)
        return one * self.NT

    def rmsnorm(self, wcol, ndim_inv, epscol):
        bk, Bbk = self.bank()
        for kt in range(8):
            sq, Bsq = self.scr()
            sqb = sq[:, 0:TT // 2].bitcast(BF16)
            if kt % 2 == 0:
                self.act(sqb, self.xT[:, kt, :], AF.Square, [self.Bx[kt]], [Bsq])
            else:
                self.tt("dve", sqb, self.xT[:, kt, :], self.xT[:, kt, :], ALU.mult, [self.Bx[kt]], [Bsq])
            self.mm(bk[:, :], self.cmb[:, 1152:1280], sqb, kt == 0, kt == 7, [Bsq, self.Bc], [Bbk])
        rs, Brs = self.scr()
        self.act(rs[:, 0:TT], bk[:, :], AF.Ln, [Bbk, self.Bc], [Brs], scale=ndim_inv, bias=self.cv[:, epscol:epscol + 1])
        self.act(rs[:, 0:TT], rs[:, 0:TT], AF.Exp, [Brs], [Brs], scale=-0.5)
        return rs, Brs

    def norm_apply_h(self, wcol, rs, Brs):
        for kt in range(8):
            self.stt(self.hT[:, kt, :], self.xT[:, kt, :], self.cv[:, wcol + kt:wcol + kt + 1], rs[:, 0:TT],
                     ALU.mult, ALU.mult, [self.Bx[kt], Brs, self.Bc], [self.Bh[kt]])

    def resid_add(self, m, bk, Bbk):
        self.tt("dve", self.xT[:, m, :], self.xT[:, m, :], bk[:, :], ALU.add, [self.Bx[m], Bbk], [self.Bx[m]])

    def stage_l0(self, t):
        P = self.P
        tok0 = t * TT
        sbn = ["pooled%d" % g for g in range(4)] + ["qr%d" % j for j in range(4)] + ["mix%d" % k for k in range(8)] + \
              ["pT0", "pT1", "pT2", "pT3", "cos", "sin", "pi"] + ["u%d" % g for g in range(4)]
        SB = self.stage_bufs(sbn)
        o = self.carve(4 * 528 * 4); self.ubuf = self.a_f32(o, 4 * 528).rearrange("p (g t) -> p g t", g=4)
        self.Bu = [SB["u%d" % g] for g in range(4)]
        for g in range(4):
            self.cp("pool", self.ubuf[:, g, 0:16], self.utail[:, g, :], [self.Butail], [self.Bu[g]])
        o = self.carve(4 * TT * 2); pooledT = self.a_bf(o, 4 * TT).rearrange("p (g t) -> p g t", g=4)
        o = self.carve(4 * TT * 2); qrT = self.a_bf(o, 4 * TT).rearrange("p (g t) -> p g t", g=4)
        o = self.carve(8 * TT * 2); mixT = self.a_bf(o, 8 * TT).rearrange("p (g t) -> p g t", g=8)
        pT = []
        for i in range(4):
            o = self.carve(TT * 2); pT.append(self.a_bf(o, TT))
        o = self.carve(TT * 4); cosT = self.a_f32(o, TT)
        o = self.carve(TT * 4); sinT = self.a_f32(o, TT)
        o = self.carve(TT * 4); pi = self.a_f32(o, TT).bitcast(I32)

        P.dma("sp", pi, self.pos_d[0:1, tok0:tok0 + TT].partition_broadcast(128), writes=[SB["pi"]], slot=8)
        rs, Brs = self.rmsnorm(C_NM0, 1.0 / D, C_EPS)
        self.norm_apply_h(C_NM0, rs, Brs)

        a, Ba = self.scr(); k, Bk = self.scr(); ki, Bki = self.scr()
        A = a[:, 0:TT]; Kf = k[:, 0:TT]; Ki = ki[:, 0:TT].bitcast(I32)
        C1 = 6.28125; C2 = float(2 * np.pi - 6.28125); PI = float(np.pi)
        self.cp("dve", A, pi, [SB["pi"]], [Ba])
        self.ts("dve", A, A, self.cv[:, C_INVF:C_INVF + 1], None, ALU.mult, None, [Ba, self.Bc], [Ba])
        self.ts("dve", Kf, A, float(1 / (2 * np.pi)), None, ALU.mult, None, [Ba], [Bk])
        self.cp("dve", Ki, Kf, [Bk], [Bki])
        self.cp("dve", Kf, Ki, [Bki], [Bk])
        self.stt(A, Kf, -C1, A, ALU.mult, ALU.add, [Bk, Ba], [Ba])
        self.stt(A, Kf, -C2, A, ALU.mult, ALU.add, [Bk, Ba], [Ba])

        def wrap(X, BX):
            self.P.op("dve", lambda e: e.tensor_single_scalar(out=Kf, in_=X, scalar=PI, op=ALU.is_gt), [BX], [Bk])
            self.stt(X, Kf, -2 * PI, X, ALU.mult, ALU.add, [Bk, BX], [BX])
            self.P.op("dve", lambda e: e.tensor_single_scalar(out=Kf, in_=X, scalar=-PI, op=ALU.is_lt), [BX], [Bk])
            self.stt(X, Kf, 2 * PI, X, ALU.mult, ALU.add, [Bk, BX], [BX])
        wrap(A, Ba)
        self.act(sinT, A, AF.Sin, [Ba, self.Bc], [SB["sin"]], scale=self.cv[:, C_SGN:C_SGN + 1])
        self.ts("dve", A, A, PI / 2, None, ALU.add, None, [Ba], [Ba])
        wrap(A, Ba)
        self.act(cosT, A, AF.Sin, [Ba], [SB["cos"]])


        W, BW = self.wget(8, 512)
        for g in range(4):
            bk, Bbk = self.bank()
            for kt in range(8):
                self.mm(bk[:, :], W[:, kt, g * 128:(g + 1) * 128], self.hT[:, kt, :], kt == 0, kt == 7, [BW, self.Bh[kt]], [Bbk])
            self.cp("act", self.ubuf[:, g, 16:16 + TT], bk[:, :], [Bbk], [self.Bu[g]])

        def rope(bq, Bq, bp, Bp, out, Bout):
            t1, Bt1 = self.scr(); t2, Bt2 = self.scr()
            self.tt("dve", t1[:, 0:TT], bq[:, :], cosT, ALU.mult, [Bq, SB["cos"]], [Bt1])
            self.tt("dve", t2[:, 0:TT], bp[:, :], sinT, ALU.mult, [Bp, SB["sin"]], [Bt2])
            self.tt("dve", out, t1[:, 0:TT], t2[:, 0:TT], ALU.add, [Bt1, Bt2], [Bout])
        for half in range(2):
            W, BW = self.wget(8, 512)
            for jj in range(2):
                j = half * 2 + jj
                bq, Bq = self.bank(); bp, Bp = self.bank()
                for kt in range(8):
                    self.mm(bq[:, :], W[:, kt, jj * 256:jj * 256 + 128], self.hT[:, kt, :], kt == 0, kt == 7, [BW, self.Bh[kt]], [Bq])
                for kt in range(8):
                    self.mm(bp[:, :], W[:, kt, jj * 256 + 128:jj * 256 + 256], self.hT[:, kt, :], kt == 0, kt == 7, [BW, self.Bh[kt]], [Bp])
                rope(bq, Bq, bp, Bp, qrT[:, j, :], SB["qr%d" % j])
        W, BW = self.wget(8, 384)
        bq, Bq = self.bank(); bp, Bp = self.bank()
        for kt in range(8):
            self.mm(bq[:, :], W[:, kt, 0:128], self.hT[:, kt, :], kt == 0, kt == 7, [BW, self.Bh[kt]], [Bq])
        for kt in range(8):
            self.mm(bp[:, :], W[:, kt, 128:256], self.hT[:, kt, :], kt == 0, kt == 7, [BW, self.Bh[kt]], [Bp])
        rope(bq, Bq, bp, Bp, self.krT[:, 128:128 + TT], self.Bkr)
        bv, Bbv = self.bank()
        for c in range(4):
            for kt in range(8):
                self.mm(bv[:, c * 128:(c + 1) * 128], self.hT[:, kt, c * 128:(c + 1) * 128], W[:, kt, 256:384], kt == 0, kt == 7,
                        [BW, self.Bh[kt]], [Bbv])
        self.cp("act", self.vbuf[:, 1:5, :], bv[:, :].rearrange("p (c d) -> p c d", c=4), [Bbv], [self.Bv])

        for g in range(4):
            w = 2 << g
            src = self.ubuf[:, g, :]
            Bsrc = self.Bu[g]
            lo = 0
            sh = 1
            for step in range(g + 1):
                dst, Bdst = self.scr()
                nlo = lo + sh
                self.tt("pool", dst[:, nlo:528], src[:, nlo:528], src[:, nlo - sh:528 - sh], ALU.add, [Bsrc], [Bdst])
                src, Bsrc, lo, sh = dst, Bdst, nlo, sh * 2
            self.stt(pooledT[:, g, :], src[:, 16:528], 1.0 / w, self.ubuf[:, g, 16:528], ALU.mult, ALU.subtract,
                     [Bsrc, self.Bu[g]], [SB["pooled%d" % g]])
            if t == 0:
                tmp, Btmp = self.scr()
                self.tt("dve", tmp[:, 0:16], src[:, 16:32], self.cv[:, C_ICNT + g * 16:C_ICNT + g * 16 + 16], ALU.mult, [Bsrc, self.Bc], [Btmp])
                self.tt("dve", pooledT[:, g, 0:16], tmp[:, 0:16], self.ubuf[:, g, 16:32], ALU.subtract, [Btmp, self.Bu[g]], [SB["pooled%d" % g]])
            self.cp("pool", self.utail[:, g, :], self.ubuf[:, g, TT:TT + 16], [self.Bu[g]], [self.Butail])

        pti = [0]
        Brq = [SB["qr%d" % j] for j in range(4)]

        def att_s(c, g):
            cs = slice(c * 128, (c + 1) * 128)
            ps = slice(g * 64, (g + 1) * 64)
            first = (t == 0 and c == 0)
            rq = qrT[ps, :, cs]
            blocks = []
            if not first:
                blocks.append((slice(c * 128, (c + 1) * 128), self.cmb[:, 128:640], c))
            blocks.append((slice((c + 1) * 128, (c + 2) * 128), self.cmb[:, 640:1152], c + 1))
            pts = []
            for (kc, mb, vi) in blocks:
                bs, Bbs = self.bank()
                self.mm(bs[:, :], self.krT[ps, kc], rq, True, False, [self.Bkr] + Brq, [Bbs])
                self.mm(bs[:, :], self.cmb[:, 0:128], mb, False, True, [self.Bc], [Bbs])
                p_ = pT[pti[0] % 4]; Bp_ = SB["pT%d" % (pti[0] % 4)]; pti[0] += 1
                self.act(p_, bs[:, :], AF.Exp, [Bbs], [Bp_], scale=0.125)
                pts.append((p_, Bp_, vi))
            return pts

        cur_od = {}

        def att_v(c, g, pts):
            cs = slice(c * 128, (c + 1) * 128)
            ps = slice(g * 64, (g + 1) * 64)
            if g == 0:
                cur_od[c] = (self.bank(), self.bank())
            (bO, BbO), (bD, BbD) = cur_od[c]
            for i, (p_, Bp_, vi) in enumerate(pts):
                self.mm(bO[ps, :], self.vbuf[:, vi, ps], p_, i == 0, i == len(pts) - 1, [self.Bv, Bp_], [BbO])
            for i, (p_, Bp_, vi) in enumerate(pts):
                self.mm(bD[ps, :], self.cmb[:, 1152:1216], p_, i == 0, i == len(pts) - 1, [self.Bc, Bp_], [BbD])
            if g == 1:
                den, Bden = self.scr()
                den3 = den[:, 0:TT].rearrange("p (j q) -> p j q", j=4)
                self.tt("dve", den3, bD[:, :].rearrange("p (j q) -> p j q", j=4),
                        self.esink[:, 0:4].unsqueeze(2).broadcast_to([128, 4, 128]), ALU.add, [BbD, self.Besink], [Bden])
                self.act(den[:, 0:TT], den[:, 0:TT], AF.Ln, [Bden], [Bden])
                self.act(den[:, 0:TT], den[:, 0:TT], AF.Exp, [Bden], [Bden], scale=-1.0)
                self.tt("dve", mixT[:, 4:8, cs], bO[:, :].rearrange("p (j q) -> p j q", j=4), den3, ALU.mult,
                        [BbO, Bden], [SB["mix%d" % k_] for k_ in range(4, 8)])

        units = [(c, g) for c in range(4) for g in range(2)]
        nxt = att_s(*units[0])
        for i, (c, g) in enumerate(units):
            pts = nxt
            if i + 1 < len(units):
                nxt = att_s(*units[i + 1])
            att_v(c, g, pts)
        for g in range(4):
            bk, Bbk = self.bank()
            self.mm(bk[:, :], self.pw[:, g, :], pooledT[:, g, :], True, True, [self.Bc, SB["pooled%d" % g]], [Bbk])
            self.act(mixT[:, g, :], bk[:, :], AF.Copy, [Bbk, self.Bc], [SB["mix%d" % g]], scale=self.cv[:, C_PSC + g:C_PSC + g + 1])
        self.cp("pool", self.krT[:, 0:128], self.krT[:, TT:TT + 128], [self.Bkr], [self.Bkr])
        self.cp("pool", self.vbuf[:, 0, :], self.vbuf[:, 4, :], [self.Bv], [self.Bv])

        for half in range(2):
            W, BW = self.wget(8, 512)
            for mm_ in range(4):
                m = half * 4 + mm_
                bk, Bbk = self.bank()
                for kt in range(8):
                    self.mm(bk[:, :], W[:, kt, mm_ * 128:(mm_ + 1) * 128], mixT[:, kt, :], kt == 0, kt == 7, [BW, SB["mix%d" % kt]], [Bbk])
                self.resid_add(m, bk, Bbk)

    def conv_multi(self, tiles, car, Bcar, t, K):
        rd, wr = t % 2, (t + 1) % 2
        accs = []
        for (bk, Bbk, ct, wcol, bcol) in tiles:
            acc, Bacc = self.scr()
            self.act(acc[:, 0:TT], bk[:, :], AF.Identity, [Bbk, self.Bc], [Bacc], scale=wcol(K - 1), bias=bcol)
            self.cp("act", car[:, wr, ct, :], bk[:, TT - (K - 1):TT], [Bbk], [Bcar[wr]])
            accs.append((acc, Bacc))
        for s_ in range(1, K):
            k_ = K - 1 - s_
            for (bk, Bbk, ct, wcol, bcol), (acc, Bacc) in zip(tiles, accs):
                self.stt(acc[:, s_:TT], bk[:, 0:TT - s_], wcol(k_), acc[:, s_:TT], ALU.mult, ALU.add, [Bbk, Bacc, self.Bc], [Bacc])
        for s_ in range(1, K):
            k_ = K - 1 - s_
            for (bk, Bbk, ct, wcol, bcol), (acc, Bacc) in zip(tiles, accs):
                self.stt(acc[:, 0:s_], car[:, rd, ct, K - 1 - s_:K - 1], wcol(k_), acc[:, 0:s_], ALU.mult, ALU.add,
                         [Bcar[rd], Bacc, self.Bc], [Bacc])
        return accs

    def stage_ffn(self, t, l):
        SB = self.stage_bufs(["act%d" % j for j in range(22)])
        o = self.carve(22 * TT * 2)
        actT = self.a_bf(o, 22 * TT).rearrange("p (j t) -> p j t", j=22)
        wn = C_NF0 if l == 0 else C_NF1
        rs, Brs = self.rmsnorm(wn, 1.0 / D, C_EPS)
        self.norm_apply_h(wn, rs, Brs)
        car = self.fcar[l]
        Bcar = self.Bfcar[l]
        pend_ffn = None

        def finish(j, accs):
            (au, Bau), (ag, Bag) = accs
            self.act(ag[:, 0:TT], ag[:, 0:TT], AF.Silu, [Bag], [Bag])
            self.tt("pool", actT[:, j, :], ag[:, 0:TT], au[:, 0:TT], ALU.mult, [Bag, Bau], [SB["act%d" % j]])
        for b in range(11):
            W, BW = self.wget(8, 512)
            for jj in range(2):
                j = b * 2 + jj
                tiles = []
                for which in range(2):
                    ct = j + 22 * which
                    c0 = jj * 256 + which * 128
                    bk, Bbk = self.bank()
                    for kt in range(8):
                        self.mm(bk[:, :], W[:, kt, c0:c0 + 128], self.hT[:, kt, :], kt == 0, kt == 7, [BW, self.Bh[kt]], [Bbk])
                    wc = (lambda ct_: (lambda k_: self.cv[:, C_FCW + (l * 3 + k_) * 44 + ct_:C_FCW + (l * 3 + k_) * 44 + ct_ + 1]))(ct)
                    bc = self.cv[:, C_FCB + l * 44 + ct:C_FCB + l * 44 + ct + 1]
                    tiles.append((bk, Bbk, ct, wc, bc))
                accs = self.conv_multi(tiles, car, Bcar, t, 3)
                if pend_ffn is not None:
                    finish(*pend_ffn)
                pend_ffn = (j, accs)
        finish(*pend_ffn)
        for m in range(8):
            W, BW = self.wget(22, 128)
            bk, Bbk = self.bank()
            for kt in range(22):
                self.mm(bk[:, :], W[:, kt, :], actT[:, kt, :], kt == 0, kt == 21, [BW, SB["act%d" % kt]], [Bbk])
            self.resid_add(m, bk, Bbk)

    def stage_ssd(self, t):
        P = self.P
        names = ["xbc%d" % i for i in range(32)] + ["sz%d" % c for c in range(4)] + ["ynT0", "ynT1", "xdt", "xdd", "btok", "y", "yn",
                                                                                     "M0", "M1", "AU0", "AU1", "XD"]
        SB = self.stage_bufs(names)
        o = self.carve(32 * TT * 2); xbcT = self.a_bf(o, 32 * TT).rearrange("p (c t) -> p c t", c=32)
        o = self.carve(4 * 2048 * 2); sz = self.a_bf(o, 4 * 2048).rearrange("p (c f) -> p c f", c=4)
        o = self.carve(16 * TT * 2); ynT = self.a_bf(o, 16 * TT).rearrange("p (k t) -> p k t", k=16)
        o = self.carve(2048 * 2); xdt = self.a_bf(o, 2048)
        o = self.carve(2048 * 2); xdd = self.a_bf(o, 2048)
        o = self.carve(1024 * 2); btok = self.a_bf(o, 1024)
        o = self.carve(2048 * 4); y = self.a_f32(o, 2048)
        o = self.carve(2048 * 2); yn = self.a_bf(o, 2048)
        Ms = []
        AUs = []
        for i in range(2):
            o = self.carve(512 * 2); Ms.append(self.a_bf(o, 512).rearrange("p (h l) -> p h l", h=4))
        for i in range(2):
            o = self.carve(512 * 4); AUs.append(self.a_f32(o, 512).rearrange("p (h l) -> p h l", h=4))
        o = self.carve(2048 * 2); XD = self.a_bf(o, 2048)

        rs, Brs = self.rmsnorm(C_NM1, 1.0 / D, C_EPS)
        self.norm_apply_h(C_NM1, rs, Brs)

        def z_block(b):
            W, BW = self.wget(8, 512)
            for c in range(4):
                bk, Bbk = self.bank()
                for kt in range(8):
                    self.mm(bk[:, :], self.hT[:, kt, c * 128:(c + 1) * 128], W[:, kt, :], kt == 0, kt == 7, [BW, self.Bh[kt]], [Bbk])
                self.act(sz[:, c, b * 512:(b + 1) * 512], bk[:, :], AF.Silu, [Bbk], [SB["sz%d" % c]])
        pend_x = None

        def fin_x(tiles, accs):
            for (bk, Bbk, ct, wc, bc), (acc, Bacc) in zip(tiles, accs):
                self.act(xbcT[:, ct, :], acc[:, 0:TT], AF.Silu, [Bacc], [SB["xbc%d" % ct]])
        for b in range(8):
            W, BW = self.wget(8, 512)
            for i2 in range(2):
                tiles = []
                for ii in range(2):
                    i = i2 * 2 + ii
                    ct = b * 4 + i
                    bk, Bbk = self.bank()
                    for kt in range(8):
                        self.mm(bk[:, :], W[:, kt, i * 128:(i + 1) * 128], self.hT[:, kt, :], kt == 0, kt == 7, [BW, self.Bh[kt]], [Bbk])
                    wc = (lambda ct_: (lambda k_: self.cv[:, C_SCW + k_ * 32 + ct_:C_SCW + k_ * 32 + ct_ + 1]))(ct)
                    bc = self.cv[:, C_SCB + ct:C_SCB + ct + 1]
                    tiles.append((bk, Bbk, ct, wc, bc))
                accs = self.conv_multi(tiles, self.scar, self.Bscar, t, 4)
                if pend_x is not None:
                    fin_x(*pend_x)
                pend_x = (tiles, accs)
            if b % 2 == 1:
                z_block(b // 2)
        fin_x(*pend_x)
        W, BW = self.wget(8, 32)
        bk, Bbk = self.bank()
        for c in range(4):
            for kt in range(8):
                self.mm(bk[:, c * 32:(c + 1) * 32], self.hT[:, kt, c * 128:(c + 1) * 128], W[:, kt, :], kt == 0, kt == 7, [BW, self.Bh[kt]], [Bbk])
        dtA = self.dtA; aA = self.aA
        dt3 = dtA[:, :].rearrange("p (c h) -> p c h", c=4)
        self.tt("dve", dt3, bk[:, 0:128].rearrange("p (c h) -> p c h", c=4),
                self.rows[:, R_DTB:R_DTB + 32].unsqueeze(1).broadcast_to([128, 4, 32]), ALU.add, [Bbk, self.Bc], [self.Bdt])
        self.ts("dve", dtA[:, :], dtA[:, :], 30.0, None, ALU.min, None, [self.Bdt], [self.Bdt])
        self.act(dtA[:, :], dtA[:, :], AF.Exp, [self.Bdt], [self.Bdt])
        self.act(dtA[:, :], dtA[:, :], AF.Ln, [self.Bdt], [self.Bdt], bias=1.0)
        self.tt("dve", aA[:, :].rearrange("p (c h) -> p c h", c=4), dt3,
                self.Arow[:, 0:32].unsqueeze(1).broadcast_to([128, 4, 32]), ALU.mult, [self.Bdt, self.BArow], [self.Ba])

        identb = self.cmb[:, 0:128]
        tri = self.cm[:, 128:256]
        Umat = self.cm[:, 256:384]
        ones = self.cm[:, 384:512]

        def smv(c):
            sm = self.ssm_small[:, c % 2, :]
            return dict(sm=sm, B=self.Bsm[c % 2], ea=sm[:, 0:32], acs=sm[:, 32:64], dec=sm[:, 64:96], cd=sm[:, 96:128],
                        dd=sm[:, 128:160], ddt=sm[:, 160:192], ss=sm[:, 192:193],
                        a=aA[:, c * 32:(c + 1) * 32], dt=dtA[:, c * 32:(c + 1) * 32], cs=slice(c * 128, (c + 1) * 128))

        def p_small(c):
            v = smv(c); Bsm = v["B"]; cs = v["cs"]; a_c = v["a"]; dt_c = v["dt"]
            b1, Bb1 = self.bank()
            self.mm(b1[:, 0:32], tri, a_c, True, True, [self.Bc, self.Ba], [Bb1])
            self.mm(b1[:, 32:64], ones, a_c, True, True, [self.Bc, self.Ba], [Bb1])
            self.act(v["ea"], b1[:, 0:32], AF.Exp, [Bb1], [Bsm])
            self.act(v["acs"], b1[:, 0:32], AF.Copy, [Bb1], [Bsm])
            self.act(v["cd"], b1[:, 32:64], AF.Exp, [Bb1], [Bsm])
            self.tt("dve", v["dec"], b1[:, 32:64], v["acs"], ALU.subtract, [Bb1, Bsm], [Bsm])
            self.act(v["dec"], v["dec"], AF.Exp, [Bsm], [Bsm])
            self.tt("dve", v["dd"], v["dec"], dt_c, ALU.mult, [Bsm, self.Bdt], [Bsm])
            self.recip(v["ddt"], dt_c, [self.Bdt, Bsm], [Bsm])
            self.tt("dve", v["ddt"], v["ddt"], self.rows[:, R_D:R_D + 32], ALU.mult, [Bsm, self.Bc], [Bsm])

        def p_big(c):
            v = smv(c); Bsm = v["B"]; cs = v["cs"]; a_c = v["a"]; dt_c = v["dt"]
            for half in range(2):
                tb, Btb = self.tbank()
                for i in range(8):
                    ct = half * 8 + i
                    self.tr(tb[:, i * 128:(i + 1) * 128], xbcT[:, ct, cs], identb, [SB["xbc%d" % ct], self.Bc], [Btb])
                hs = slice(half * 16, (half + 1) * 16)
                tb3 = tb[:, :].rearrange("p (h d) -> p h d", h=16)
                self.tt("dve", xdt[:, half * 1024:(half + 1) * 1024].rearrange("p (h d) -> p h d", h=16), tb3,
                        dt_c[:, hs].unsqueeze(2).broadcast_to([128, 16, 64]), ALU.mult, [Btb, self.Bdt], [SB["xdt"]])
                self.tt("dve", xdd[:, half * 1024:(half + 1) * 1024].rearrange("p (h d) -> p h d", h=16), tb3,
                        v["dd"][:, hs].unsqueeze(2).broadcast_to([128, 16, 64]), ALU.mult, [Btb, Bsm], [SB["xdd"]])
                self.tt("dve", XD[:, half * 1024:(half + 1) * 1024].rearrange("p (h d) -> p h d", h=16), tb3,
                        self.rows[:, R_D + half * 16:R_D + (half + 1) * 16].unsqueeze(2).broadcast_to([128, 16, 64]),
                        ALU.mult, [Btb, self.Bc], [SB["XD"]])
            tb, Btb = self.tbank()
            for i in range(8):
                self.tr(tb[:, i * 128:(i + 1) * 128], xbcT[:, 16 + i, cs], identb, [SB["xbc%d" % (16 + i)], self.Bc], [Btb])
            self.cp("act", btok, tb[:, :], [Btb], [SB["btok"]])

        pend = {}

        def grp_a1(c, g):
            v = smv(c); cs = v["cs"]; h0 = 4 * g
            Bt_ = xbcT[:, 16 + g, cs]; Ct_ = xbcT[:, 24 + g, cs]
            BBt = SB["xbc%d" % (16 + g)]; BCt = SB["xbc%d" % (24 + g)]
            bcb, Bbcb = self.bank()
            self.mm(bcb[:, 0:128], Bt_, Ct_, True, True, [BBt, BCt], [Bbcb])
            cbm, Bcbm = self.scr()
            self.tt("dve", cbm[:, 0:128], bcb[:, 0:128], tri, ALU.mult, [Bbcb, self.Bc], [Bcbm])
            AU = AUs[g % 2]; BAU = SB["AU%d" % (g % 2)]
            self.tt("pool", AU, Umat.unsqueeze(1).broadcast_to([128, 4, 128]),
                    v["a"][:, h0:h0 + 4].unsqueeze(2).broadcast_to([128, 4, 128]), ALU.mult, [self.Bc, self.Ba], [BAU])
            bs, Bbs = self.bank()
            for hh in range(4):
                self.mm(bs[:, hh * 128:(hh + 1) * 128], AU[:, hh, :], tri, True, True, [BAU, self.Bc], [Bbs])
            pend[(c, g)] = (cbm, Bcbm, bs, Bbs)

        def grp_a2(c, g):
            cbm, Bcbm, bs, Bbs = pend.pop((c, g))
            E, BE = self.scr()
            self.act(E[:, 0:TT], bs[:, :], AF.Exp, [Bbs], [BE])
            M = Ms[g % 2]; BM = SB["M%d" % (g % 2)]
            self.tt("dve", M, E[:, 0:TT].rearrange("p (h l) -> p h l", h=4),
                    cbm[:, 0:128].unsqueeze(1).broadcast_to([128, 4, 128]), ALU.mult, [BE, Bcbm], [BM])

        def grp_b(c, g):
            v = smv(c); Bsm = v["B"]; cs = v["cs"]; h0 = 4 * g
            Ct_ = xbcT[:, 24 + g, cs]; BCt = SB["xbc%d" % (24 + g)]
            M = Ms[g % 2]; BM = SB["M%d" % (g % 2)]
            by, Bby = self.bank()
            self.mm(by[:, 0:256], identb, XD[:, g * 256:(g + 1) * 256], True, False, [self.Bc, SB["XD"]], [Bby])
            for hh in range(4):
                h = h0 + hh
                self.mm(by[:, hh * 64:(hh + 1) * 64], M[:, hh, :], xdt[:, h * 64:(h + 1) * 64], False, hh == 3, [BM, SB["xdt"]], [Bby])
            self.mm(by[:, 256:512], Ct_, self.Sbf[:, g * 256:(g + 1) * 256], True, True, [BCt, self.BSbf[g]], [Bby])
            bst, Bbst = self.bank()
            self.mm(bst[:, 0:256], btok[:, g * 128:(g + 1) * 128], xdd[:, g * 256:(g + 1) * 256], True, True, [SB["btok"], SB["xdd"]], [Bbst])
            t1, Bt1 = self.scr()
            self.tt("dve", t1[:, 0:256].rearrange("p (h d) -> p h d", h=4), by[:, 256:512].rearrange("p (h d) -> p h d", h=4),
                    v["ea"][:, h0:h0 + 4].unsqueeze(2).broadcast_to([128, 4, 64]), ALU.mult, [Bby, Bsm], [Bt1])
            self.tt("dve", y[:, g * 256:(g + 1) * 256], by[:, 0:256], t1[:, 0:256], ALU.add, [Bby, Bt1], [SB["y"]])
            Sg = self.S[:, g * 256:(g + 1) * 256]
            self.tt("pool", Sg.rearrange("p (h d) -> p h d", h=4), Sg.rearrange("p (h d) -> p h d", h=4),
                    v["cd"][:, h0:h0 + 4].unsqueeze(2).broadcast_to([128, 4, 64]), ALU.mult, [self.BS[g], Bsm], [self.BS[g]])
            self.tt("dve", Sg, Sg, bst[:, 0:256], ALU.add, [self.BS[g], Bbst], [self.BS[g]])

        def sbf_copy(g):
            self.cp("act", self.Sbf[:, g * 256:(g + 1) * 256], self.S[:, g * 256:(g + 1) * 256], [self.BS[g]], [self.BSbf[g]])

        def e_elem(c):
            v = smv(c); Bsm = v["B"]; cs = v["cs"]; ss = v["ss"]
            self.tt("pool", y, y, sz[:, c, :], ALU.mult, [SB["y"], SB["sz%d" % c]], [SB["y"]])
            self.P.op("dve", lambda e: e.memset(ss, 0.0), [], [Bsm])
            self.act(yn, y, AF.Square, [SB["y"]], [SB["yn"], Bsm], accum_out=ss)
            self.act(ss, ss, AF.Ln, [Bsm, self.Bc], [Bsm], scale=1.0 / 2048, bias=self.cv[:, C_EPS2:C_EPS2 + 1])
            self.act(ss, ss, AF.Exp, [Bsm], [Bsm], scale=-0.5)
            self.stt(yn, y, ss, self.rows[:, R_NW:R_NW + 2048], ALU.mult, ALU.mult, [SB["y"], Bsm, self.Bc], [SB["yn"]])

        def e_tr(c):
            v = smv(c); cs = v["cs"]
            for half in range(2):
                tb, Btb = self.tbank()
                for i in range(8):
                    kt = half * 8 + i
                    self.tr(tb[:, i * 128:(i + 1) * 128], yn[:, kt * 128:(kt + 1) * 128], identb, [SB["yn"], self.Bc], [Btb])
                self.cp("act", ynT[:, half * 8:(half + 1) * 8, cs], tb[:, :].rearrange("p (k t) -> p k t", k=8), [Btb], [SB["ynT%d" % half]])

        p_small(0)
        grp_a1(0, 0)
        grp_a1(0, 1)
        grp_a2(0, 0)
        p_big(0)
        for c in range(4):
            for g in range(8):
                if g + 2 < 8:
                    grp_a1(c, g + 2)
                if g + 1 < 8:
                    grp_a2(c, g + 1)
                if g >= 2:
                    sbf_copy(g - 2)
                grp_b(c, g)
                if g == 4 and c > 0:
                    e_tr(c - 1)
                if g == 5 and c + 1 < 4:
                    p_small(c + 1)
            sbf_copy(6)
            if c + 1 < 4:
                grp_a1(c + 1, 0)
                grp_a1(c + 1, 1)
                grp_a2(c + 1, 0)
                p_big(c + 1)
            sbf_copy(7)
            e_elem(c)
        e_tr(3)
        for m in range(8):
            W, BW = self.wget(16, 128)
            bk, Bbk = self.bank()
            for kt in range(16):
                self.mm(bk[:, :], W[:, kt, :], ynT[:, kt, :], kt == 0, kt == 15, [BW, SB["ynT%d" % (kt // 8)]], [Bbk])
            self.resid_add(m, bk, Bbk)

    def stage_final(self, t):
        tok0 = t * TT
        rs, Brs = self.rmsnorm(C_NFIN, 1.0 / D, C_EPS)
        for kt in range(8):
            o, Bo = self.scr()
            self.stt(o[:, 0:TT], self.xT[:, kt, :], self.cv[:, C_NFIN + kt:C_NFIN + kt + 1], rs[:, 0:TT],
                     ALU.mult, ALU.mult, [self.Bx[kt], Brs, self.Bc], [Bo])
            tok = self.P.dma("sp", self.out_d[kt * 128:(kt + 1) * 128, tok0:tok0 + TT], o[:, 0:TT], reads=[Bo], slot=16 + kt)
            self.out_toks.append(tok)
            if t + 1 < self.NT:
                self.P.dma("sp", self.xT[:, kt, :], self.xT_d[kt * 128:(kt + 1) * 128, tok0 + TT:tok0 + 2 * TT],
                           writes=[self.Bx[kt]], slot=28 + kt)

    def dbg_dump(self, t):
        tok0 = t * TT
        for kt in range(8):
            tok = self.P.dma("sp", self.out_d[kt * 128:(kt + 1) * 128, tok0:tok0 + TT], self.xT[:, kt, :], reads=[self.Bx[kt]], slot=16 + kt)
            self.out_toks.append(tok)

    def build(self):
        nc = self.nc
        S = self.S
        dr = lambda n, shp, dt=F32, kind="ExternalInput": nc.dram_tensor(n, shp, dt, kind=kind).ap()
        self.xT_d = dr("xT", [D, S])
        self.pos_d = dr("pos", [1, S], I32)
        self.cvec_d = dr("cvec", [128, NCV])
        self.rows_d = dr("rows", [1, NR])
        self.cmat_d = dr("cmat", [128, 512])
        self.cmatb_d = dr("cmatb", [128, 1280])
        self.poolw_d = dr("poolw", [128, 512])
        self.win0a = dr("win0a", [3 * 128, 4096])
        self.win0b = dr("win0b", [128, 3072])
        self.wout0 = dr("wout0", [2 * 128, 4096])
        self.fup = [dr("fup%d" % l, [11 * 128, 4096]) for l in range(2)]
        self.fdn = [dr("fdn%d" % l, [8 * 128, 2816]) for l in range(2)]
        self.sina = dr("sina", [12 * 128, 4096])
        self.sinb = dr("sinb", [128, 256])
        self.sout = dr("sout", [8 * 128, 2048])
        self.out_d = dr("outT", [D, S], F32, "ExternalOutput")
        with ExitStack() as st:
            P = self.P = Prog(nc, st)
            sb = lambda n, shp, dt: st.enter_context(nc.sbuf_tensor(n, shp, dt))
            self.xT = sb("xT_s", [128, 8, TT], F32)
            self.hT = sb("hT_s", [128, 8, TT], BF16)
            self.Bx = [Buf("x%d" % k) for k in range(8)]
            self.Bh = [Buf("h%d" % k) for k in range(8)]
            self.cv = sb("cv", [128, NCV], F32)
            self.rows = sb("rows_s", [128, NR], F32)
            self.cm = sb("cm", [128, 512], F32)
            self.cmb = sb("cmb", [128, 1280], BF16)
            self.pw = sb("pw", [128, 4, 128], BF16)
            self.Bc = Buf("consts")
            self.Arow = sb("Arow", [128, 32], F32); self.BArow = Buf("Arow")
            self.esink = sb("esink", [128, 4], F32); self.Besink = Buf("esink")
            self.utail = sb("utail", [128, 4, 16], F32); self.Butail = Buf("utail")
            self.krT = sb("krT", [128, 128 + TT], BF16); self.Bkr = Buf("kr")
            self.vbuf = sb("vbuf", [128, 5, 128], BF16); self.Bv = Buf("v")
            self.fcar = [sb("fcar%d" % l, [128, 2, 44, 2], F32) for l in range(2)]
            self.Bfcar = [[Buf("fcar%d_%d" % (l, q)) for q in range(2)] for l in range(2)]
            self.scar = sb("scar", [128, 2, 32, 3], F32); self.Bscar = [Buf("scar%d" % q) for q in range(2)]
            self.S = sb("S_s", [128, 2048], F32); self.BS = [Buf("S%d" % g) for g in range(8)]
            self.Sbf = sb("Sbf", [128, 2048], BF16); self.BSbf = [Buf("Sbf%d" % g) for g in range(8)]
            self.dtA = sb("dtA", [128, 128], F32); self.Bdt = Buf("dt")
            self.aA = sb("aA", [128, 128], F32); self.Ba = Buf("a")
            self.ssm_small = sb("ssm_small", [128, 2, 256], F32); self.Bsm = [Buf("sm0"), Buf("sm1")]
            self.NSLOT = 4
            self.wslot = [sb("wslot%d" % i, [128, 4096], BF16) for i in range(self.NSLOT)]
            self.Bw = [Buf("w%d" % i) for i in range(self.NSLOT)]
            NSCR = 9
            self.scrs = [(sb("scr%d" % i, [128, 528], F32), Buf("scr%d" % i)) for i in range(NSCR)]
            self.scr_i = 0
            self.ARENA_BYTES = 96 * 1024
            self.arena = sb("arena", [128, self.ARENA_BYTES // 4], F32)
            self.arena_users = []
            self.arena_off = 0
            self.banks = [(st.enter_context(nc.psum_tensor("pb%d" % i, [128, 512], F32)), Buf("pb%d" % i)) for i in range(6)]
            self.bank_i = 0
            self.tbanks = [(st.enter_context(nc.psum_tensor("tb%d" % i, [128, 1024], BF16)), Buf("tb%d" % i)) for i in range(2)]
            self.tbank_i = 0
            self.wsched = self.make_wsched()
            self.nblk_tile = len(self.wsched) // self.NT
            self.USE_WSCR = True
            self.wscr = nc.dram_tensor("wscr", [self.nblk_tile * 128, 4096], BF16, kind="Internal").ap()
            self.Bwscr = [Buf("wscr%d" % i) for i in range(self.nblk_tile)]
            self.w_cur = 0
            self.w_issued = 0
            self.out_toks = []

            ctoks = [P.dma("sp", self.cv[:, :], self.cvec_d, writes=[self.Bc], slot=4),
                     P.dma("sp", self.rows[:, :], self.rows_d.partition_broadcast(128), writes=[self.Bc], slot=5),
                     P.dma("sp", self.cm[:, :], self.cmat_d, writes=[self.Bc], slot=6),
                     P.dma("pool", self.cmb[:, :], self.cmatb_d, writes=[self.Bc], slot=7),
                     P.dma("pool", self.pw[:, :, :], self.poolw_d.rearrange("p (g d) -> p g d", g=4), writes=[self.Bc], slot=15)]
            for e_ in ("pe", "act", "dve", "pool"):
                P.wait_all(e_, ctoks)
            self.act(self.Arow[:, :], self.rows[:, R_ALOG:R_ALOG + 32], AF.Exp, [self.Bc], [self.BArow])
            self.ts("dve", self.Arow[:, :], self.Arow[:, :], -1.0, None, ALU.mult, None, [self.BArow], [self.BArow])
            self.act(self.esink[:, :], self.cv[:, C_SINK:C_SINK + 4], AF.Exp, [self.Bc], [self.Besink])
            P.op("dve", lambda e: e.memset(self.utail[:, :, :], 0.0), [], [self.Butail])
            P.op("dve", lambda e: e.memset(self.krT[:, :], 0.0), [], [self.Bkr])
            P.op("dve", lambda e: e.memset(self.vbuf[:, :, :], 0.0), [], [self.Bv])
            for l in range(2):
                P.op("pool", lambda e, l=l: e.memset(self.fcar[l][:, :, :, :], 0.0), [], self.Bfcar[l])
            P.op("pool", lambda e: e.memset(self.scar[:, :, :, :], 0.0), [], self.Bscar)
            P.op("pool", lambda e: e.memset(self.S[:, :], 0.0), [], self.BS)
            P.op("pool", lambda e: e.memset(self.Sbf[:, :], 0.0), [], self.BSbf)

            for t in range(self.NT):
                tok0 = t * TT
                if t == 0 or self.dbg_stage < 4:
                    for kt in range(8):
                        P.dma("sp", self.xT[:, kt, :], self.xT_d[kt * 128:(kt + 1) * 128, tok0:tok0 + TT], writes=[self.Bx[kt]], slot=28 + kt)
                nst = 0
                for fn in (lambda: self.stage_l0(t), lambda: self.stage_ffn(t, 0), lambda: self.stage_ssd(t), lambda: self.stage_ffn(t, 1)):
                    if nst < self.dbg_stage:
                        fn()
                    else:
                        pass
                    nst += 1
                if self.dbg_stage >= 4:
                    self.stage_final(t)
                else:
                    self.dbg_dump(t)
            P.wait_all("sp", self.out_toks)
            P.emit()
        return nc


def _col(v):
    v = np.asarray(v, np.float32)
    return np.ascontiguousarray(v.reshape(-1, 128).T)


def prep_shared(inp):
    f = lambda a: np.asarray(a, np.float32)
    cv = np.zeros((128, NCV), np.float32)
    cv[:, C_NM0:C_NM0 + 8] = _col(f(inp["norm_mix"])[0])
    cv[:, C_NF0:C_NF0 + 8] = _col(f(inp["norm_ffn"])[0])
    cv[:, C_NM1:C_NM1 + 8] = _col(f(inp["norm_mix"])[1])
    cv[:, C_NF1:C_NF1 + 8] = _col(f(inp["norm_ffn"])[1])
    cv[:, C_NFIN:C_NFIN + 8] = _col(f(inp["norm_final"]))
    cv[:, C_PSC:C_PSC + 4] = _col(f(inp["pool_scale"])[0])
    fcw = f(inp["ffn_conv_w"]); fcb = f(inp["ffn_conv_b"])
    for l in range(2):
        for k in range(3):
            c0 = C_FCW + (l * 3 + k) * 44
            cv[:, c0:c0 + 44] = _col(fcw[l, k])
        cv[:, C_FCB + l * 44:C_FCB + l * 44 + 44] = _col(fcb[l])
    scw = f(inp["ssm_conv_w"])[0]; scb = f(inp["ssm_conv_b"])[0]
    for k in range(4):
        cv[:, C_SCW + k * 32:C_SCW + k * 32 + 32] = _col(scw[k])
    cv[:, C_SCB:C_SCB + 32] = _col(scb)
    inv_freq = (np.float32(10000.0) ** (-np.arange(0, 64, 2, dtype=np.float32) / np.float32(64))).astype(np.float32)
    p = np.arange(128)
    cv[:, C_INVF] = inv_freq[p % 32]
    cv[:, C_SGN] = np.where((p % 64) < 32, -1.0, 1.0)
    sinks = f(inp["attn_sinks"])[0]
    for j in range(4):
        cv[:, C_SINK + j] = np.where(p < 64, sinks[j], sinks[4 + j])
    for g in range(4):
        w = 2 << g
        for tt_ in range(16):
            cv[:, C_ICNT + g * 16 + tt_] = 1.0 / min(tt_ + 1, w)
    cv[:, C_EPS] = 1e-6
    cv[:, C_EPS2] = 1e-5
    rows = np.zeros((1, NR), np.float32)
    rows[0, R_NW:R_NW + 2048] = f(inp["ssm_norm"])[0]
    rows[0, R_D:R_D + 32] = f(inp["ssm_D"])[0]
    rows[0, R_ALOG:R_ALOG + 32] = f(inp["ssm_A_log"])[0]
    rows[0, R_DTB:R_DTB + 32] = f(inp["ssm_dt_bias"])[0]
    i = np.arange(128)
    ident = np.eye(128, dtype=np.float32)
    tri = (i[:, None] <= i[None, :]).astype(np.float32)
    U = (i[None, :] < i[:, None]).astype(np.float32)
    cmat = np.concatenate([ident, tri, U, np.ones((128, 128), np.float32)], axis=1)
    mprev = np.where(i[:, None] > i[None, :], 0.0, NEG).astype(np.float32)
    mcur = np.where(i[:, None] <= i[None, :], 0.0, NEG).astype(np.float32)
    cmatb = np.concatenate([ident, np.tile(mprev, (1, 4)), np.tile(mcur, (1, 4)), np.ones((128, 128), np.float32)], axis=1)
    w_in = f(inp["mix_w_in"])[0]
    u_c = np.arange(0, 512)
    perm64 = np.concatenate([np.arange(32, 64), np.arange(0, 32)])
    qcols = []
    for j in range(4):
        hA = np.concatenate([512 + j * 64 + np.arange(64), 512 + (4 + j) * 64 + np.arange(64)])
        hP = np.concatenate([512 + j * 64 + perm64, 512 + (4 + j) * 64 + perm64])
        qcols += [hA, hP]
    kc = 1024 + np.arange(128)
    kp = np.concatenate([1024 + perm64, 1024 + 64 + perm64])
    vc = 1152 + np.arange(128)
    cols = np.concatenate([u_c] + qcols + [kc, kp, vc])
    win0 = np.ascontiguousarray(w_in[:, cols])
    w_out = f(inp["mix_w_out"])[0]
    rws = [np.arange(512)]
    for j in range(4):
        rws.append(np.concatenate([512 + j * 64 + np.arange(64), 512 + (4 + j) * 64 + np.arange(64)]))
    wout0 = np.ascontiguousarray(w_out[np.concatenate(rws), :])
    poolw = np.ascontiguousarray(f(inp["pool_w"])[0].transpose(1, 0, 2).reshape(128, 512))
    fup = []
    for l in range(2):
        wu = f(inp["ffn_w_up"])[l]
        wi = np.stack([wu[:, :DFF].reshape(D, 22, 128), wu[:, DFF:].reshape(D, 22, 128)], axis=2)
        fup.append(np.ascontiguousarray(wi.reshape(D, 2 * DFF)))
    def blockify(W, c0, nb, nk, ncol):
        Wb = W[:, c0:c0 + nb * ncol].reshape(nk, 128, nb, ncol).transpose(2, 1, 0, 3)
        return np.ascontiguousarray(Wb.reshape(nb * 128, nk * ncol))
    w_sin = f(inp["ssm_w_in"])[0]
    sh = {"cvec": cv, "rows": rows, "cmat": cmat, "cmatb": cmatb, "poolw": poolw,
          "win0a": blockify(win0, 0, 3, 8, 512), "win0b": blockify(win0, 1536, 1, 8, 384),
          "wout0": blockify(wout0, 0, 2, 8, 512),
          "fup0": blockify(fup[0], 0, 11, 8, 512), "fup1": blockify(fup[1], 0, 11, 8, 512),
          "fdn0": blockify(f(inp["ffn_w_down"])[0], 0, 8, 22, 128), "fdn1": blockify(f(inp["ffn_w_down"])[1], 0, 8, 22, 128),
          "sina": blockify(w_sin, 0, 12, 8, 512), "sinb": blockify(w_sin, 6144, 1, 8, 32),
          "sout": blockify(f(inp["ssm_w_out"])[0], 0, 8, 16, 128)}
    return sh


_CACHE = {}


def run(inp, S=SEQ, n_cores=4, dbg_stage=99):
    x = np.asarray(inp["x"], np.float32)
    pos = np.asarray(inp["positions"], np.int32)
    NT = S // TT
    key = (S, dbg_stage)
    if key not in _CACHE:
        _CACHE[key] = Builder(S, NT, dbg_stage).build()
    nc = _CACHE[key]
    sh = prep_shared(inp)
    in_maps = []
    for c in range(n_cores):
        b = c % x.shape[0]
        m = dict(sh)
        m["xT"] = np.ascontiguousarray(x[b, :S].T)
        m["pos"] = np.ascontiguousarray(pos[b:b + 1, :S])
        in_maps.append(m)
    res = run_bass_kernel_spmd(nc, in_maps, core_ids=list(range(n_cores)))
    outs = [np.ascontiguousarray(res.results[c]["outT"].T) for c in range(n_cores)]
    return outs


def kernel(**inputs):
    outs = run(inputs, SEQ, 4)
    return np.stack(outs[:4], axis=0).astype(np.float32)
```

```python
import numpy as np
import concourse.bass as bass
import concourse.mybir as mybir
from concourse.bass_utils import run_bass_kernel_spmd
from contextlib import ExitStack

F32 = mybir.dt.float32
BF16 = mybir.dt.bfloat16
I32 = mybir.dt.int32
ALU = mybir.AluOpType
AF = mybir.ActivationFunctionType

D = 1024
SEQ = 4096
TT = 512
DFF = 2816
NEG = -30000.0

C_NM0, C_NF0, C_NM1, C_NF1, C_NFIN, C_PSC = 0, 8, 16, 24, 32, 40
C_FCW = 44
C_FCB = 308
C_SCW = 396
C_SCB = 524
C_INVF, C_SGN, C_SINK = 556, 557, 558
C_ICNT = 562
C_EPS, C_EPS2 = 626, 627
NCV = 628
R_NW, R_D, R_ALOG, R_DTB = 0, 2048, 2080, 2112
NR = 2144


class Buf:
    __slots__ = ("name", "lw", "rd")

    def __init__(self, name):
        self.name = name
        self.lw = None
        self.rd = []


class Prog:
    ENGS = ("pe", "act", "dve", "pool", "sp")

    def __init__(self, nc, stack, n_dma_sems=36):
        self.nc = nc
        self.streams = {e: [] for e in self.ENGS}
        self.cnt = {e: 0 for e in self.ENGS}
        self.sem = {e: stack.enter_context(nc.semaphore("s_" + e)) for e in self.ENGS}
        self.seen = {e: {} for e in self.ENGS}
        self.dsem = [stack.enter_context(nc.semaphore("d%d" % i)) for i in range(n_dma_sems)]
        self.dcnt = [0] * n_dma_sems
        self.dlast = [None] * n_dma_sems
        self.snaps = {}

    def _need(self, eng, tokens):
        for tok in tokens:
            if tok is None:
                continue
            sem, val, src = tok
            if src == "pe" and eng == "pe":
                continue
            key = id(sem)
            if self.seen[eng].get(key, 0) >= val:
                continue
            self.seen[eng][key] = val
            self.streams[eng].append(("wait", sem, val))
            snap = self.snaps.get((key, val))
            if snap:
                mine = self.seen[eng]
                for k2, v2 in snap.items():
                    if mine.get(k2, 0) < v2:
                        mine[k2] = v2

    @staticmethod
    def _compact(toks):
        best = {}
        for t in toks:
            if t is None:
                continue
            k = id(t[0])
            if k not in best or best[k][1] < t[1]:
                best[k] = t
        return list(best.values())

    def _deps(self, reads, writes):
        toks = []
        for b in reads:
            toks.append(b.lw)
        for b in writes:
            toks.append(b.lw)
            toks.extend(b.rd)
        return toks

    def _commit(self, tok, reads, writes):
        for b in reads:
            b.rd.append(tok)
            if len(b.rd) > 12:
                b.rd = self._compact(b.rd)
        for b in writes:
            b.lw = tok
            b.rd = []

    def op(self, eng, fn, reads=(), writes=()):
        self._need(eng, self._deps(reads, writes))
        self.cnt[eng] += 1
        tok = (self.sem[eng], self.cnt[eng], eng)
        self.snaps[(id(self.sem[eng]), self.cnt[eng])] = dict(self.seen[eng])
        self.streams[eng].append(("op", fn, self.sem[eng], 1))
        self._commit(tok, reads, writes)
        return tok

    def dma(self, q, out, in_, reads=(), writes=(), slot=0):
        self._need(q, self._deps(reads, writes) + [self.dlast[slot]])
        self.dcnt[slot] += 16
        sem = self.dsem[slot]
        tok = (sem, self.dcnt[slot], "dma")
        self.snaps[(id(sem), self.dcnt[slot])] = dict(self.seen[q])
        self.streams[q].append(("op", (lambda e: e.dma_start(out=out, in_=in_)), sem, 16))
        self.dlast[slot] = tok
        self._commit(tok, reads, writes)
        return tok

    def wait_all(self, eng, toks):
        self._need(eng, toks)

    def alias(self, new_bufs, old_bufs):
        toks = []
        for b in old_bufs:
            toks.append(b.lw)
            toks.extend(b.rd)
        toks = self._compact(toks)
        for nb in new_bufs:
            nb.lw = None
            nb.rd = list(toks)

    def emit(self):
        nc = self.nc
        with nc.Block() as block:
            def run(e):
                def body(eng):
                    for item in self.streams[e]:
                        if item[0] == "wait":
                            eng.wait_ge(item[1], item[2])
                        else:
                            item[1](eng).then_inc(item[2], item[3])
                return body
            block.tensor(run("pe"))
            block.scalar(run("act"))
            block.vector(run("dve"))
            block.gpsimd(run("pool"))
            block.sync(run("sp"))


class Builder:
    def __init__(self, S, NT, dbg_stage=99):
        self.S = S
        self.NT = NT
        self.dbg_stage = dbg_stage
        self.nc = bass.Bass("TRN2", target_bir_lowering=False)

    def mm(self, out, lhsT, rhs, start, stop, reads, writes):
        self.P.op("pe", lambda e: e.matmul(out, lhsT=lhsT, rhs=rhs, start=start, stop=stop), reads, writes)

    def tr(self, out, in_, ident, reads, writes):
        self.P.op("pe", lambda e: e.transpose(out, in_, ident), reads, writes)

    def act(self, out, in_, func, reads, writes, **kw):
        self.P.op("act", lambda e: e.activation(out=out, in_=in_, func=func, **kw), reads, writes)

    def tt(self, eng, out, in0, in1, op, reads, writes):
        self.P.op(eng, lambda e: e.tensor_tensor(out=out, in0=in0, in1=in1, op=op), reads, writes)

    def ts(self, eng, out, in0, s1, s2, op0, op1, reads, writes):
        if s2 is None:
            self.P.op(eng, lambda e: e.tensor_scalar(out=out, in0=in0, scalar1=s1, scalar2=None, op0=op0), reads, writes)
        else:
            self.P.op(eng, lambda e: e.tensor_scalar(out=out, in0=in0, scalar1=s1, scalar2=s2, op0=op0, op1=op1), reads, writes)

    def stt(self, out, in0, scalar, in1, op0, op1, reads, writes):
        self.P.op("dve", lambda e: e.scalar_tensor_tensor(out=out, in0=in0, scalar=scalar, in1=in1, op0=op0, op1=op1), reads, writes)

    def cp(self, eng, out, in_, reads, writes):
        if eng == "act":
            self.act(out, in_, AF.Copy, reads, writes)
        else:
            self.P.op(eng, lambda e: e.tensor_copy(out=out, in_=in_), reads, writes)

    def recip(self, out, in_, reads, writes):
        self.P.op("dve", lambda e: e.reciprocal(out=out, in_=in_), reads, writes)

    def bank(self):
        i = self.bank_i
        self.bank_i = (i + 1) % len(self.banks)
        return self.banks[i]

    def tbank(self):
        i = self.tbank_i
        self.tbank_i = (i + 1) % len(self.tbanks)
        return self.tbanks[i]

    def scr(self):
        i = self.scr_i
        self.scr_i = (i + 1) % len(self.scrs)
        return self.scrs[i]

    def carve(self, nbytes):
        off = self.arena_off
        assert off % 4 == 0
        self.arena_off += (nbytes + 3) // 4 * 4
        assert self.arena_off <= self.ARENA_BYTES, (self.arena_off, self.ARENA_BYTES)
        return off

    def a_bf(self, off, n):
        return self.arena[:, off // 4: off // 4 + n // 2].bitcast(BF16)

    def a_f32(self, off, n):
        return self.arena[:, off // 4: off // 4 + n]

    def stage_bufs(self, names):
        bufs = {n: Buf(n) for n in names}
        self.P.alias(list(bufs.values()), self.arena_users)
        self.arena_users = list(bufs.values())
        self.arena_off = 0
        return bufs

    def wget(self, nk, ncol):
        blk = self.wsched[self.w_cur]
        assert blk[1] == nk and blk[2] == ncol, (self.w_cur, blk[1:], nk, ncol)
        hi = min(len(self.wsched), self.w_cur + self.NSLOT)
        NB = self.nblk_tile
        while self.w_issued < hi:
            i = self.w_issued
            ap_d, k_, c_ = self.wsched[i]
            s = i % self.NSLOT
            n = k_ * c_
            bi = i % NB
            scr_ap = self.wscr[bi * 128:(bi + 1) * 128, 0:n]
            if i < NB or not self.USE_WSCR:
                self.P.dma("pool", self.wslot[s][:, 0:n], ap_d, writes=[self.Bw[s]], slot=s)
                if self.USE_WSCR and self.NT > 1:
                    self.P.dma("sp", scr_ap, self.wslot[s][:, 0:n], reads=[self.Bw[s]], writes=[self.Bwscr[bi]], slot=24 + (i % 4))
            else:
                self.P.dma("sp", self.wslot[s][:, 0:n], scr_ap, reads=[self.Bwscr[bi]], writes=[self.Bw[s]], slot=s)
            self.w_issued += 1
        s = self.w_cur % self.NSLOT
        self.w_cur += 1
        return self.wslot[s][:, 0:nk * ncol].rearrange("p (k c) -> p k c", k=nk), self.Bw[s]

    def make_wsched(self):
        def blks(w, nb, nk, ncol):
            return [(w[b * 128:(b + 1) * 128, :], nk, ncol) for b in range(nb)]
        one = blks(self.win0a, 3, 8, 512) + blks(self.win0b, 1, 8, 384) + blks(self.wout0, 2, 8, 512)

        def ffn(l):
            return blks(self.fup[l], 11, 8, 512) + blks(self.fdn[l], 8, 22, 128)
        if self.dbg_stage < 1:
            one = []
        if self.dbg_stage >= 2:
            one += ffn(0)
        sa = blks(self.sina, 12, 8, 512)
        ssd = []
        for b in range(8):
            ssd.append(sa[4 + b])
            if b % 2 == 1:
                ssd.append(sa[b // 2])
        ssd += blks(self.sinb, 1, 8, 32) + blks(self.sout, 8, 16, 128)
        if self.dbg_stage >= 3:
            one += ssd
        if self.dbg_stage >= 4:
            one += ffn(1)
        return one * self.NT

    def rmsnorm(self, wcol, ndim_inv, epscol):
        bk, Bbk = self.bank()
        for kt in range(8):
            sq, Bsq = self.scr()
            sqb = sq[:, 0:TT // 2].bitcast(BF16)
            if kt % 2 == 0:
                self.act(sqb, self.xT[:, kt, :], AF.Square, [self.Bx[kt]], [Bsq])
            else:
                self.tt("dve", sqb, self.xT[:, kt, :], self.xT[:, kt, :], ALU.mult, [self.Bx[kt]], [Bsq])
            self.mm(bk[:, :], self.cmb[:, 1152:1280], sqb, kt == 0, kt == 7, [Bsq, self.Bc], [Bbk])
        rs, Brs = self.scr()
        self.act(rs[:, 0:TT], bk[:, :], AF.Ln, [Bbk, self.Bc], [Brs], scale=ndim_inv, bias=self.cv[:, epscol:epscol + 1])
        self.act(rs[:, 0:TT], rs[:, 0:TT], AF.Exp, [Brs], [Brs], scale=-0.5)
        return rs, Brs

    def norm_apply_h(self, wcol, rs, Brs):
        for kt in range(8):
            self.stt(self.hT[:, kt, :], self.xT[:, kt, :], self.cv[:, wcol + kt:wcol + kt + 1], rs[:, 0:TT],
                     ALU.mult, ALU.mult, [self.Bx[kt], Brs, self.Bc], [self.Bh[kt]])

    def resid_add(self, m, bk, Bbk):
        self.tt("dve", self.xT[:, m, :], self.xT[:, m, :], bk[:, :], ALU.add, [self.Bx[m], Bbk], [self.Bx[m]])

    def stage_l0(self, t):
        P = self.P
        tok0 = t * TT
        sbn = ["pooled%d" % g for g in range(4)] + ["qr%d" % j for j in range(4)] + ["mix%d" % k for k in range(8)] + \
              ["pT0", "pT1", "pT2", "pT3", "cos", "sin", "pi"] + ["u%d" % g for g in range(4)]
        SB = self.stage_bufs(sbn)
        o = self.carve(4 * 528 * 4); self.ubuf = self.a_f32(o, 4 * 528).rearrange("p (g t) -> p g t", g=4)
        self.Bu = [SB["u%d" % g] for g in range(4)]
        for g in range(4):
            self.cp("pool", self.ubuf[:, g, 0:16], self.utail[:, g, :], [self.Butail], [self.Bu[g]])
        o = self.carve(4 * TT * 2); pooledT = self.a_bf(o, 4 * TT).rearrange("p (g t) -> p g t", g=4)
        o = self.carve(4 * TT * 2); qrT = self.a_bf(o, 4 * TT).rearrange("p (g t) -> p g t", g=4)
        o = self.carve(8 * TT * 2); mixT = self.a_bf(o, 8 * TT).rearrange("p (g t) -> p g t", g=8)
        pT = []
        for i in range(4):
            o = self.carve(TT * 2); pT.append(self.a_bf(o, TT))
        o = self.carve(TT * 4); cosT = self.a_f32(o, TT)
        o = self.carve(TT * 4); sinT = self.a_f32(o, TT)
        o = self.carve(TT * 4); pi = self.a_f32(o, TT).bitcast(I32)

        P.dma("sp", pi, self.pos_d[0:1, tok0:tok0 + TT].partition_broadcast(128), writes=[SB["pi"]], slot=8)
        rs, Brs = self.rmsnorm(C_NM0, 1.0 / D, C_EPS)
        self.norm_apply_h(C_NM0, rs, Brs)

        a, Ba = self.scr(); k, Bk = self.scr(); ki, Bki = self.scr()
        A = a[:, 0:TT]; Kf = k[:, 0:TT]; Ki = ki[:, 0:TT].bitcast(I32)
        C1 = 6.28125; C2 = float(2 * np.pi - 6.28125); PI = float(np.pi)
        self.cp("dve", A, pi, [SB["pi"]], [Ba])
        self.ts("dve", A, A, self.cv[:, C_INVF:C_INVF + 1], None, ALU.mult, None, [Ba, self.Bc], [Ba])
        self.ts("dve", Kf, A, float(1 / (2 * np.pi)), None, ALU.mult, None, [Ba], [Bk])
        self.cp("dve", Ki, Kf, [Bk], [Bki])
        self.cp("dve", Kf, Ki, [Bki], [Bk])
        self.stt(A, Kf, -C1, A, ALU.mult, ALU.add, [Bk, Ba], [Ba])
        self.stt(A, Kf, -C2, A, ALU.mult, ALU.add, [Bk, Ba], [Ba])

        def wrap(X, BX):
            self.P.op("dve", lambda e: e.tensor_single_scalar(out=Kf, in_=X, scalar=PI, op=ALU.is_gt), [BX], [Bk])
            self.stt(X, Kf, -2 * PI, X, ALU.mult, ALU.add, [Bk, BX], [BX])
            self.P.op("dve", lambda e: e.tensor_single_scalar(out=Kf, in_=X, scalar=-PI, op=ALU.is_lt), [BX], [Bk])
            self.stt(X, Kf, 2 * PI, X, ALU.mult, ALU.add, [Bk, BX], [BX])
        wrap(A, Ba)
        self.act(sinT, A, AF.Sin, [Ba, self.Bc], [SB["sin"]], scale=self.cv[:, C_SGN:C_SGN + 1])
        self.ts("dve", A, A, PI / 2, None, ALU.add, None, [Ba], [Ba])
        wrap(A, Ba)
        self.act(cosT, A, AF.Sin, [Ba], [SB["cos"]])


        W, BW = self.wget(8, 512)
        for g in range(4):
            bk, Bbk = self.bank()
            for kt in range(8):
                self.mm(bk[:, :], W[:, kt, g * 128:(g + 1) * 128], self.hT[:, kt, :], kt == 0, kt == 7, [BW, self.Bh[kt]], [Bbk])
            self.cp("act", self.ubuf[:, g, 16:16 + TT], bk[:, :], [Bbk], [self.Bu[g]])

        def rope(bq, Bq, bp, Bp, out, Bout):
            t1, Bt1 = self.scr(); t2, Bt2 = self.scr()
            self.tt("dve", t1[:, 0:TT], bq[:, :], cosT, ALU.mult, [Bq, SB["cos"]], [Bt1])
            self.tt("dve", t2[:, 0:TT], bp[:, :], sinT, ALU.mult, [Bp, SB["sin"]], [Bt2])
            self.tt("dve", out, t1[:, 0:TT], t2[:, 0:TT], ALU.add, [Bt1, Bt2], [Bout])
        for half in range(2):
            W, BW = self.wget(8, 512)
            for jj in range(2):
                j = half * 2 + jj
                bq, Bq = self.bank(); bp, Bp = self.bank()
                for kt in range(8):
                    self.mm(bq[:, :], W[:, kt, jj * 256:jj * 256 + 128], self.hT[:, kt, :], kt == 0, kt == 7, [BW, self.Bh[kt]], [Bq])
                for kt in range(8):
                    self.mm(bp[:, :], W[:, kt, jj * 256 + 128:jj * 256 + 256], self.hT[:, kt, :], kt == 0, kt == 7, [BW, self.Bh[kt]], [Bp])
                rope(bq, Bq, bp, Bp, qrT[:, j, :], SB["qr%d" % j])
        W, BW = self.wget(8, 384)
        bq, Bq = self.bank(); bp, Bp = self.bank()
        for kt in range(8):
            self.mm(bq[:, :], W[:, kt, 0:128], self.hT[:, kt, :], kt == 0, kt == 7, [BW, self.Bh[kt]], [Bq])
        for kt in range(8):
            self.mm(bp[:, :], W[:, kt, 128:256], self.hT[:, kt, :], kt == 0, kt == 7, [BW, self.Bh[kt]], [Bp])
        rope(bq, Bq, bp, Bp, self.krT[:, 128:128 + TT], self.Bkr)
        bv, Bbv = self.bank()
        for c in range(4):
            for kt in range(8):
                self.mm(bv[:, c * 128:(c + 1) * 128], self.hT[:, kt, c * 128:(c + 1) * 128], W[:, kt, 256:384], kt == 0, kt == 7,
                        [BW, self.Bh[kt]], [Bbv])
        self.cp("act", self.vbuf[:, 1:5, :], bv[:, :].rearrange("p (c d) -> p c d", c=4), [Bbv], [self.Bv])

        for g in range(4):
            w = 2 << g
            src = self.ubuf[:, g, :]
            Bsrc = self.Bu[g]
            lo = 0
            sh = 1
            for step in range(g + 1):
                dst, Bdst = self.scr()
                nlo = lo + sh
                self.tt("pool", dst[:, nlo:528], src[:, nlo:528], src[:, nlo - sh:528 - sh], ALU.add, [Bsrc], [Bdst])
                src, Bsrc, lo, sh = dst, Bdst, nlo, sh * 2
            self.stt(pooledT[:, g, :], src[:, 16:528], 1.0 / w, self.ubuf[:, g, 16:528], ALU.mult, ALU.subtract,
                     [Bsrc, self.Bu[g]], [SB["pooled%d" % g]])
            if t == 0:
                tmp, Btmp = self.scr()
                self.tt("dve", tmp[:, 0:16], src[:, 16:32], self.cv[:, C_ICNT + g * 16:C_ICNT + g * 16 + 16], ALU.mult, [Bsrc, self.Bc], [Btmp])
                self.tt("dve", pooledT[:, g, 0:16], tmp[:, 0:16], self.ubuf[:, g, 16:32], ALU.subtract, [Btmp, self.Bu[g]], [SB["pooled%d" % g]])
            self.cp("pool", self.utail[:, g, :], self.ubuf[:, g, TT:TT + 16], [self.Bu[g]], [self.Butail])

        pti = [0]
        Brq = [SB["qr%d" % j] for j in range(4)]

        def att_s(c, g):
            cs = slice(c * 128, (c + 1) * 128)
            ps = slice(g * 64, (g + 1) * 64)
            first = (t == 0 and c == 0)
            rq = qrT[ps, :, cs]
            blocks = []
            if not first:
                blocks.append((slice(c * 128, (c + 1) * 128), self.cmb[:, 128:640], c))
            blocks.append((slice((c + 1) * 128, (c + 2) * 128), self.cmb[:, 640:1152], c + 1))
            pts = []
            for (kc, mb, vi) in blocks:
                bs, Bbs = self.bank()
                self.mm(bs[:, :], self.krT[ps, kc], rq, True, False, [self.Bkr] + Brq, [Bbs])
                self.mm(bs[:, :], self.cmb[:, 0:128], mb, False, True, [self.Bc], [Bbs])
                p_ = pT[pti[0] % 4]; Bp_ = SB["pT%d" % (pti[0] % 4)]; pti[0] += 1
                self.act(p_, bs[:, :], AF.Exp, [Bbs], [Bp_], scale=0.125)
                pts.append((p_, Bp_, vi))
            return pts

        cur_od = {}

        def att_v(c, g, pts):
            cs = slice(c * 128, (c + 1) * 128)
            ps = slice(g * 64, (g + 1) * 64)
            if g == 0:
                cur_od[c] = (self.bank(), self.bank())
            (bO, BbO), (bD, BbD) = cur_od[c]
            for i, (p_, Bp_, vi) in enumerate(pts):
                self.mm(bO[ps, :], self.vbuf[:, vi, ps], p_, i == 0, i == len(pts) - 1, [self.Bv, Bp_], [BbO])
            for i, (p_, Bp_, vi) in enumerate(pts):
                self.mm(bD[ps, :], self.cmb[:, 1152:1216], p_, i == 0, i == len(pts) - 1, [self.Bc, Bp_], [BbD])
            if g == 1:
                den, Bden = self.scr()
                den3 = den[:, 0:TT].rearrange("p (j q) -> p j q", j=4)
                self.tt("dve", den3, bD[:, :].rearrange("p (j q) -> p j q", j=4),
                        self.esink[:, 0:4].unsqueeze(2).broadcast_to([128, 4, 128]), ALU.add, [BbD, self.Besink], [Bden])
                self.act(den[:, 0:TT], den[:, 0:TT], AF.Ln, [Bden], [Bden])
                self.act(den[:, 0:TT], den[:, 0:TT], AF.Exp, [Bden], [Bden], scale=-1.0)
                self.tt("dve", mixT[:, 4:8, cs], bO[:, :].rearrange("p (j q) -> p j q", j=4), den3, ALU.mult,
                        [BbO, Bden], [SB["mix%d" % k_] for k_ in range(4, 8)])

        units = [(c, g) for c in range(4) for g in range(2)]
        nxt = att_s(*units[0])
        for i, (c, g) in enumerate(units):
            pts = nxt
            if i + 1 < len(units):
                nxt = att_s(*units[i + 1])
            att_v(c, g, pts)
        for g in range(4):
            bk, Bbk = self.bank()
            self.mm(bk[:, :], self.pw[:, g, :], pooledT[:, g, :], True, True, [self.Bc, SB["pooled%d" % g]], [Bbk])
            self.act(mixT[:, g, :], bk[:, :], AF.Copy, [Bbk, self.Bc], [SB["mix%d" % g]], scale=self.cv[:, C_PSC + g:C_PSC + g + 1])
        self.cp("pool", self.krT[:, 0:128], self.krT[:, TT:TT + 128], [self.Bkr], [self.Bkr])
        self.cp("pool", self.vbuf[:, 0, :], self.vbuf[:, 4, :], [self.Bv], [self.Bv])

        for half in range(2):
            W, BW = self.wget(8, 512)
            for mm_ in range(4):
                m = half * 4 + mm_
                bk, Bbk = self.bank()
                for kt in range(8):
                    self.mm(bk[:, :], W[:, kt, mm_ * 128:(mm_ + 1) * 128], mixT[:, kt, :], kt == 0, kt == 7, [BW, SB["mix%d" % kt]], [Bbk])
                self.resid_add(m, bk, Bbk)

    def conv_multi(self, tiles, car, Bcar, t, K):
        rd, wr = t % 2, (t + 1) % 2
        accs = []
        for (bk, Bbk, ct, wcol, bcol) in tiles:
            acc, Bacc = self.scr()
            self.act(acc[:, 0:TT], bk[:, :], AF.Identity, [Bbk, self.Bc], [Bacc], scale=wcol(K - 1), bias=bcol)
            self.cp("act", car[:, wr, ct, :], bk[:, TT - (K - 1):TT], [Bbk], [Bcar[wr]])
            accs.append((acc, Bacc))
        for s_ in range(1, K):
            k_ = K - 1 - s_
            for (bk, Bbk, ct, wcol, bcol), (acc, Bacc) in zip(tiles, accs):
                self.stt(acc[:, s_:TT], bk[:, 0:TT - s_], wcol(k_), acc[:, s_:TT], ALU.mult, ALU.add, [Bbk, Bacc, self.Bc], [Bacc])
        for s_ in range(1, K):
            k_ = K - 1 - s_
            for (bk, Bbk, ct, wcol, bcol), (acc, Bacc) in zip(tiles, accs):
                self.stt(acc[:, 0:s_], car[:, rd, ct, K - 1 - s_:K - 1], wcol(k_), acc[:, 0:s_], ALU.mult, ALU.add,
                         [Bcar[rd], Bacc, self.Bc], [Bacc])
        return accs

    def stage_ffn(self, t, l):
        SB = self.stage_bufs(["act%d" % j for j in range(22)])
        o = self.carve(22 * TT * 2)
        actT = self.a_bf(o, 22 * TT).rearrange("p (j t) -> p j t", j=22)
        wn = C_NF0 if l == 0 else C_NF1
        rs, Brs = self.rmsnorm(wn, 1.0 / D, C_EPS)
        self.norm_apply_h(wn, rs, Brs)
        car = self.fcar[l]
        Bcar = self.Bfcar[l]
        pend_ffn = None

        def finish(j, accs):
            (au, Bau), (ag, Bag) = accs
            self.act(ag[:, 0:TT], ag[:, 0:TT], AF.Silu, [Bag], [Bag])
            self.tt("pool", actT[:, j, :], ag[:, 0:TT], au[:, 0:TT], ALU.mult, [Bag, Bau], [SB["act%d" % j]])
        for b in range(11):
            W, BW = self.wget(8, 512)
            for jj in range(2):
                j = b * 2 + jj
                tiles = []
                for which in range(2):
                    ct = j + 22 * which
                    c0 = jj * 256 + which * 128
                    bk, Bbk = self.bank()
                    for kt in range(8):
                        self.mm(bk[:, :], W[:, kt, c0:c0 + 128], self.hT[:, kt, :], kt == 0, kt == 7, [BW, self.Bh[kt]], [Bbk])
                    wc = (lambda ct_: (lambda k_: self.cv[:, C_FCW + (l * 3 + k_) * 44 + ct_:C_FCW + (l * 3 + k_) * 44 + ct_ + 1]))(ct)
                    bc = self.cv[:, C_FCB + l * 44 + ct:C_FCB + l * 44 + ct + 1]
                    tiles.append((bk, Bbk, ct, wc, bc))
                accs = self.conv_multi(tiles, car, Bcar, t, 3)
                if pend_ffn is not None:
                    finish(*pend_ffn)
                pend_ffn = (j, accs)
        finish(*pend_ffn)
        for m in range(8):
            W, BW = self.wget(22, 128)
            bk, Bbk = self.bank()
            for kt in range(22):
                self.mm(bk[:, :], W[:, kt, :], actT[:, kt, :], kt == 0, kt == 21, [BW, SB["act%d" % kt]], [Bbk])
            self.resid_add(m, bk, Bbk)

    def stage_ssd(self, t):
        P = self.P
        names = ["xbc%d" % i for i in range(32)] + ["sz%d" % c for c in range(4)] + ["ynT0", "ynT1", "xdt", "xdd", "btok", "y", "yn",
                                                                                     "M0", "M1", "AU0", "AU1", "XD"]
        SB = self.stage_bufs(names)
        o = self.carve(32 * TT * 2); xbcT = self.a_bf(o, 32 * TT).rearrange("p (c t) -> p c t", c=32)
        o = self.carve(4 * 2048 * 2); sz = self.a_bf(o, 4 * 2048).rearrange("p (c f) -> p c f", c=4)
        o = self.carve(16 * TT * 2); ynT = self.a_bf(o, 16 * TT).rearrange("p (k t) -> p k t", k=16)
        o = self.carve(2048 * 2); xdt = self.a_bf(o, 2048)
        o = self.carve(2048 * 2); xdd = self.a_bf(o, 2048)
        o = self.carve(1024 * 2); btok = self.a_bf(o, 1024)
        o = self.carve(2048 * 4); y = self.a_f32(o, 2048)
        o = self.carve(2048 * 2); yn = self.a_bf(o, 2048)
        Ms = []
        AUs = []
        for i in range(2):
            o = self.carve(512 * 2); Ms.append(self.a_bf(o, 512).rearrange("p (h l) -> p h l", h=4))
        for i in range(2):
            o = self.carve(512 * 4); AUs.append(self.a_f32(o, 512).rearrange("p (h l) -> p h l", h=4))
        o = self.carve(2048 * 2); XD = self.a_bf(o, 2048)

        rs, Brs = self.rmsnorm(C_NM1, 1.0 / D, C_EPS)
        self.norm_apply_h(C_NM1, rs, Brs)

        def z_block(b):
            W, BW = self.wget(8, 512)
            for c in range(4):
                bk, Bbk = self.bank()
                for kt in range(8):
                    self.mm(bk[:, :], self.hT[:, kt, c * 128:(c + 1) * 128], W[:, kt, :], kt == 0, kt == 7, [BW, self.Bh[kt]], [Bbk])
                self.act(sz[:, c, b * 512:(b + 1) * 512], bk[:, :], AF.Silu, [Bbk], [SB["sz%d" % c]])
        pend_x = None

        def fin_x(tiles, accs):
            for (bk, Bbk, ct, wc, bc), (acc, Bacc) in zip(tiles, accs):
                self.act(xbcT[:, ct, :], acc[:, 0:TT], AF.Silu, [Bacc], [SB["xbc%d" % ct]])
        for b in range(8):
            W, BW = self.wget(8, 512)
            for i2 in range(2):
                tiles = []
                for ii in range(2):
                    i = i2 * 2 + ii
                    ct = b * 4 + i
                    bk, Bbk = self.bank()
                    for kt in range(8):
                        self.mm(bk[:, :], W[:, kt, i * 128:(i + 1) * 128], self.hT[:, kt, :], kt == 0, kt == 7, [BW, self.Bh[kt]], [Bbk])
                    wc = (lambda ct_: (lambda k_: self.cv[:, C_SCW + k_ * 32 + ct_:C_SCW + k_ * 32 + ct_ + 1]))(ct)
                    bc = self.cv[:, C_SCB + ct:C_SCB + ct + 1]
                    tiles.append((bk, Bbk, ct, wc, bc))
                accs = self.conv_multi(tiles, self.scar, self.Bscar, t, 4)
                if pend_x is not None:
                    fin_x(*pend_x)
                pend_x = (tiles, accs)
            if b % 2 == 1:
                z_block(b // 2)
        fin_x(*pend_x)
        W, BW = self.wget(8, 32)
        bk, Bbk = self.bank()
        for c in range(4):
            for kt in range(8):
                self.mm(bk[:, c * 32:(c + 1) * 32], self.hT[:, kt, c * 128:(c + 1) * 128], W[:, kt, :], kt == 0, kt == 7, [BW, self.Bh[kt]], [Bbk])
        dtA = self.dtA; aA = self.aA
        dt3 = dtA[:, :].rearrange("p (c h) -> p c h", c=4)
        self.tt("dve", dt3, bk[:, 0:128].rearrange("p (c h) -> p c h", c=4),
                self.rows[:, R_DTB:R_DTB + 32].unsqueeze(1).broadcast_to([128, 4, 32]), ALU.add, [Bbk, self.Bc], [self.Bdt])
        self.ts("dve", dtA[:, :], dtA[:, :], 30.0, None, ALU.min, None, [self.Bdt], [self.Bdt])
        self.act(dtA[:, :], dtA[:, :], AF.Exp, [self.Bdt], [self.Bdt])
        self.act(dtA[:, :], dtA[:, :], AF.Ln, [self.Bdt], [self.Bdt], bias=1.0)
        self.tt("dve", aA[:, :].rearrange("p (c h) -> p c h", c=4), dt3,
                self.Arow[:, 0:32].unsqueeze(1).broadcast_to([128, 4, 32]), ALU.mult, [self.Bdt, self.BArow], [self.Ba])

        identb = self.cmb[:, 0:128]
        tri = self.cm[:, 128:256]
        Umat = self.cm[:, 256:384]
        ones = self.cm[:, 384:512]

        def smv(c):
            sm = self.ssm_small[:, c % 2, :]
            return dict(sm=sm, B=self.Bsm[c % 2], ea=sm[:, 0:32], acs=sm[:, 32:64], dec=sm[:, 64:96], cd=sm[:, 96:128],
                        dd=sm[:, 128:160], ddt=sm[:, 160:192], ss=sm[:, 192:193],
                        a=aA[:, c * 32:(c + 1) * 32], dt=dtA[:, c * 32:(c + 1) * 32], cs=slice(c * 128, (c + 1) * 128))

        def p_small(c):
            v = smv(c); Bsm = v["B"]; cs = v["cs"]; a_c = v["a"]; dt_c = v["dt"]
            b1, Bb1 = self.bank()
            self.mm(b1[:, 0:32], tri, a_c, True, True, [self.Bc, self.Ba], [Bb1])
            self.mm(b1[:, 32:64], ones, a_c, True, True, [self.Bc, self.Ba], [Bb1])
            self.act(v["ea"], b1[:, 0:32], AF.Exp, [Bb1], [Bsm])
            self.act(v["acs"], b1[:, 0:32], AF.Copy, [Bb1], [Bsm])
            self.act(v["cd"], b1[:, 32:64], AF.Exp, [Bb1], [Bsm])
            self.tt("dve", v["dec"], b1[:, 32:64], v["acs"], ALU.subtract, [Bb1, Bsm], [Bsm])
            self.act(v["dec"], v["dec"], AF.Exp, [Bsm], [Bsm])
            self.tt("dve", v["dd"], v["dec"], dt_c, ALU.mult, [Bsm, self.Bdt], [Bsm])
            self.recip(v["ddt"], dt_c, [self.Bdt, Bsm], [Bsm])
            self.tt("dve", v["ddt"], v["ddt"], self.rows[:, R_D:R_D + 32], ALU.mult, [Bsm, self.Bc], [Bsm])

        def p_big(c):
            v = smv(c); Bsm = v["B"]; cs = v["cs"]; a_c = v["a"]; dt_c = v["dt"]
            for half in range(2):
                tb, Btb = self.tbank()
                for i in range(8):
                    ct = half * 8 + i
                    self.tr(tb[:, i * 128:(i + 1) * 128], xbcT[:, ct, cs], identb, [SB["xbc%d" % ct], self.Bc], [Btb])
                hs = slice(half * 16, (half + 1) * 16)
                tb3 = tb[:, :].rearrange("p (h d) -> p h d", h=16)
                self.tt("dve", xdt[:, half * 1024:(half + 1) * 1024].rearrange("p (h d) -> p h d", h=16), tb3,
                        dt_c[:, hs].unsqueeze(2).broadcast_to([128, 16, 64]), ALU.mult, [Btb, self.Bdt], [SB["xdt"]])
                self.tt("dve", xdd[:, half * 1024:(half + 1) * 1024].rearrange("p (h d) -> p h d", h=16), tb3,
                        v["dd"][:, hs].unsqueeze(2).broadcast_to([128, 16, 64]), ALU.mult, [Btb, Bsm], [SB["xdd"]])
                self.tt("dve", XD[:, half * 1024:(half + 1) * 1024].rearrange("p (h d) -> p h d", h=16), tb3,
                        self.rows[:, R_D + half * 16:R_D + (half + 1) * 16].unsqueeze(2).broadcast_to([128, 16, 64]),
                        ALU.mult, [Btb, self.Bc], [SB["XD"]])
            tb, Btb = self.tbank()
            for i in range(8):
                self.tr(tb[:, i * 128:(i + 1) * 128], xbcT[:, 16 + i, cs], identb, [SB["xbc%d" % (16 + i)], self.Bc], [Btb])
            self.cp("act", btok, tb[:, :], [Btb], [SB["btok"]])

        pend = {}

        def grp_a1(c, g):
            v = smv(c); cs = v["cs"]; h0 = 4 * g
            Bt_ = xbcT[:, 16 + g, cs]; Ct_ = xbcT[:, 24 + g, cs]
            BBt = SB["xbc%d" % (16 + g)]; BCt = SB["xbc%d" % (24 + g)]
            bcb, Bbcb = self.bank()
            self.mm(bcb[:, 0:128], Bt_, Ct_, True, True, [BBt, BCt], [Bbcb])
            cbm, Bcbm = self.scr()
            self.tt("dve", cbm[:, 0:128], bcb[:, 0:128], tri, ALU.mult, [Bbcb, self.Bc], [Bcbm])
            AU = AUs[g % 2]; BAU = SB["AU%d" % (g % 2)]
            self.tt("pool", AU, Umat.unsqueeze(1).broadcast_to([128, 4, 128]),
                    v["a"][:, h0:h0 + 4].unsqueeze(2).broadcast_to([128, 4, 128]), ALU.mult, [self.Bc, self.Ba], [BAU])
            bs, Bbs = self.bank()
            for hh in range(4):
                self.mm(bs[:, hh * 128:(hh + 1) * 128], AU[:, hh, :], tri, True, True, [BAU, self.Bc], [Bbs])
            pend[(c, g)] = (cbm, Bcbm, bs, Bbs)

        def grp_a2(c, g):
            cbm, Bcbm, bs, Bbs = pend.pop((c, g))
            E, BE = self.scr()
            self.act(E[:, 0:TT], bs[:, :], AF.Exp, [Bbs], [BE])
            M = Ms[g % 2]; BM = SB["M%d" % (g % 2)]
            self.tt("dve", M, E[:, 0:TT].rearrange("p (h l) -> p h l", h=4),
                    cbm[:, 0:128].unsqueeze(1).broadcast_to([128, 4, 128]), ALU.mult, [BE, Bcbm], [BM])

        def grp_b(c, g):
            v = smv(c); Bsm = v["B"]; cs = v["cs"]; h0 = 4 * g
            Ct_ = xbcT[:, 24 + g, cs]; BCt = SB["xbc%d" % (24 + g)]
            M = Ms[g % 2]; BM = SB["M%d" % (g % 2)]
            by, Bby = self.bank()
            self.mm(by[:, 0:256], identb, XD[:, g * 256:(g + 1) * 256], True, False, [self.Bc, SB["XD"]], [Bby])
            for hh in range(4):
                h = h0 + hh
                self.mm(by[:, hh * 64:(hh + 1) * 64], M[:, hh, :], xdt[:, h * 64:(h + 1) * 64], False, hh == 3, [BM, SB["xdt"]], [Bby])
            self.mm(by[:, 256:512], Ct_, self.Sbf[:, g * 256:(g + 1) * 256], True, True, [BCt, self.BSbf[g]], [Bby])
            bst, Bbst = self.bank()
            self.mm(bst[:, 0:256], btok[:, g * 128:(g + 1) * 128], xdd[:, g * 256:(g + 1) * 256], True, True, [SB["btok"], SB["xdd"]], [Bbst])
            t1, Bt1 = self.scr()
            self.tt("dve", t1[:, 0:256].rearrange("p (h d) -> p h d", h=4), by[:, 256:512].rearrange("p (h d) -> p h d", h=4),
                    v["ea"][:, h0:h0 + 4].unsqueeze(2).broadcast_to([128, 4, 64]), ALU.mult, [Bby, Bsm], [Bt1])
            self.tt("dve", y[:, g * 256:(g + 1) * 256], by[:, 0:256], t1[:, 0:256], ALU.add, [Bby, Bt1], [SB["y"]])
            Sg = self.S[:, g * 256:(g + 1) * 256]
            self.tt("pool", Sg.rearrange("p (h d) -> p h d", h=4), Sg.rearrange("p (h d) -> p h d", h=4),
                    v["cd"][:, h0:h0 + 4].unsqueeze(2).broadcast_to([128, 4, 64]), ALU.mult, [self.BS[g], Bsm], [self.BS[g]])
            self.tt("dve", Sg, Sg, bst[:, 0:256], ALU.add, [self.BS[g], Bbst], [self.BS[g]])

        def sbf_copy(g):
            self.cp("act", self.Sbf[:, g * 256:(g + 1) * 256], self.S[:, g * 256:(g + 1) * 256], [self.BS[g]], [self.BSbf[g]])

        def e_elem(c):
            v = smv(c); Bsm = v["B"]; cs = v["cs"]; ss = v["ss"]
            self.tt("dve", y, y, sz[:, c, :], ALU.mult, [SB["y"], SB["sz%d" % c]], [SB["y"]])
            self.P.op("dve", lambda e: e.memset(ss, 0.0), [], [Bsm])
            self.act(yn, y, AF.Square, [SB["y"]], [SB["yn"], Bsm], accum_out=ss)
            self.act(ss, ss, AF.Ln, [Bsm, self.Bc], [Bsm], scale=1.0 / 2048, bias=self.cv[:, C_EPS2:C_EPS2 + 1])
            self.act(ss, ss, AF.Exp, [Bsm], [Bsm], scale=-0.5)
            self.stt(yn, y, ss, self.rows[:, R_NW:R_NW + 2048], ALU.mult, ALU.mult, [SB["y"], Bsm, self.Bc], [SB["yn"]])

        def e_tr(c):
            v = smv(c); cs = v["cs"]
            for half in range(2):
                tb, Btb = self.tbank()
                for i in range(8):
                    kt = half * 8 + i
                    self.tr(tb[:, i * 128:(i + 1) * 128], yn[:, kt * 128:(kt + 1) * 128], identb, [SB["yn"], self.Bc], [Btb])
                self.cp("act", ynT[:, half * 8:(half + 1) * 8, cs], tb[:, :].rearrange("p (k t) -> p k t", k=8), [Btb], [SB["ynT%d" % half]])

        p_small(0)
        grp_a1(0, 0)
        grp_a1(0, 1)
        grp_a2(0, 0)
        p_big(0)
        for c in range(4):
            for g in range(8):
                if g + 2 < 8:
                    grp_a1(c, g + 2)
                if g + 1 < 8:
                    grp_a2(c, g + 1)
                if g >= 2:
                    sbf_copy(g - 2)
                grp_b(c, g)
                if g == 4 and c > 0:
                    e_tr(c - 1)
                if g == 5 and c + 1 < 4:
                    p_small(c + 1)
            sbf_copy(6)
            if c + 1 < 4:
                grp_a1(c + 1, 0)
                grp_a1(c + 1, 1)
                grp_a2(c + 1, 0)
                p_big(c + 1)
            sbf_copy(7)
            e_elem(c)
        e_tr(3)
        for m in range(8):
            W, BW = self.wget(16, 128)
            bk, Bbk = self.bank()
            for kt in range(16):
                self.mm(bk[:, :], W[:, kt, :], ynT[:, kt, :], kt == 0, kt == 15, [BW, SB["ynT%d" % (kt // 8)]], [Bbk])
            self.resid_add(m, bk, Bbk)

    def stage_final(self, t):
        tok0 = t * TT
        rs, Brs = self.rmsnorm(C_NFIN, 1.0 / D, C_EPS)
        for kt in range(8):
            o, Bo = self.scr()
            self.stt(o[:, 0:TT], self.xT[:, kt, :], self.cv[:, C_NFIN + kt:C_NFIN + kt + 1], rs[:, 0:TT],
                     ALU.mult, ALU.mult, [self.Bx[kt], Brs, self.Bc], [Bo])
            tok = self.P.dma("sp", self.out_d[kt * 128:(kt + 1) * 128, tok0:tok0 + TT], o[:, 0:TT], reads=[Bo], slot=16 + kt)
            self.out_toks.append(tok)
            if t + 1 < self.NT:
                self.P.dma("sp", self.xT[:, kt, :], self.xT_d[kt * 128:(kt + 1) * 128, tok0 + TT:tok0 + 2 * TT],
                           writes=[self.Bx[kt]], slot=28 + kt)

    def dbg_dump(self, t):
        tok0 = t * TT
        for kt in range(8):
            tok = self.P.dma("sp", self.out_d[kt * 128:(kt + 1) * 128, tok0:tok0 + TT], self.xT[:, kt, :], reads=[self.Bx[kt]], slot=16 + kt)
            self.out_toks.append(tok)

    def build(self):
        nc = self.nc
        S = self.S
        dr = lambda n, shp, dt=F32, kind="ExternalInput": nc.dram_tensor(n, shp, dt, kind=kind).ap()
        self.xT_d = dr("xT", [D, S])
        self.pos_d = dr("pos", [1, S], I32)
        self.cvec_d = dr("cvec", [128, NCV])
        self.rows_d = dr("rows", [1, NR])
        self.cmat_d = dr("cmat", [128, 512])
        self.cmatb_d = dr("cmatb", [128, 1280])
        self.poolw_d = dr("poolw", [128, 512])
        self.win0a = dr("win0a", [3 * 128, 4096])
        self.win0b = dr("win0b", [128, 3072])
        self.wout0 = dr("wout0", [2 * 128, 4096])
        self.fup = [dr("fup%d" % l, [11 * 128, 4096]) for l in range(2)]
        self.fdn = [dr("fdn%d" % l, [8 * 128, 2816]) for l in range(2)]
        self.sina = dr("sina", [12 * 128, 4096])
        self.sinb = dr("sinb", [128, 256])
        self.sout = dr("sout", [8 * 128, 2048])
        self.out_d = dr("outT", [D, S], F32, "ExternalOutput")
        with ExitStack() as st:
            P = self.P = Prog(nc, st)
            sb = lambda n, shp, dt: st.enter_context(nc.sbuf_tensor(n, shp, dt))
            self.xT = sb("xT_s", [128, 8, TT], F32)
            self.hT = sb("hT_s", [128, 8, TT], BF16)
            self.Bx = [Buf("x%d" % k) for k in range(8)]
            self.Bh = [Buf("h%d" % k) for k in range(8)]
            self.cv = sb("cv", [128, NCV], F32)
            self.rows = sb("rows_s", [128, NR], F32)
            self.cm = sb("cm", [128, 512], F32)
            self.cmb = sb("cmb", [128, 1280], BF16)
            self.pw = sb("pw", [128, 4, 128], BF16)
            self.Bc = Buf("consts")
            self.Arow = sb("Arow", [128, 32], F32); self.BArow = Buf("Arow")
            self.esink = sb("esink", [128, 4], F32); self.Besink = Buf("esink")
            self.utail = sb("utail", [128, 4, 16], F32); self.Butail = Buf("utail")
            self.krT = sb("krT", [128, 128 + TT], BF16); self.Bkr = Buf("kr")
            self.vbuf = sb("vbuf", [128, 5, 128], BF16); self.Bv = Buf("v")
            self.fcar = [sb("fcar%d" % l, [128, 2, 44, 2], F32) for l in range(2)]
            self.Bfcar = [[Buf("fcar%d_%d" % (l, q)) for q in range(2)] for l in range(2)]
            self.scar = sb("scar", [128, 2, 32, 3], F32); self.Bscar = [Buf("scar%d" % q) for q in range(2)]
            self.S = sb("S_s", [128, 2048], F32); self.BS = [Buf("S%d" % g) for g in range(8)]
            self.Sbf = sb("Sbf", [128, 2048], BF16); self.BSbf = [Buf("Sbf%d" % g) for g in range(8)]
            self.dtA = sb("dtA", [128, 128], F32); self.Bdt = Buf("dt")
            self.aA = sb("aA", [128, 128], F32); self.Ba = Buf("a")
            self.ssm_small = sb("ssm_small", [128, 2, 256], F32); self.Bsm = [Buf("sm0"), Buf("sm1")]
            self.NSLOT = 4
            self.wslot = [sb("wslot%d" % i, [128, 4096], BF16) for i in range(self.NSLOT)]
            self.Bw = [Buf("w%d" % i) for i in range(self.NSLOT)]
            NSCR = 9
            self.scrs = [(sb("scr%d" % i, [128, 528], F32), Buf("scr%d" % i)) for i in range(NSCR)]
            self.scr_i = 0
            self.ARENA_BYTES = 96 * 1024
            self.arena = sb("arena", [128, self.ARENA_BYTES // 4], F32)
            self.arena_users = []
            self.arena_off = 0
            self.banks = [(st.enter_context(nc.psum_tensor("pb%d" % i, [128, 512], F32)), Buf("pb%d" % i)) for i in range(6)]
            self.bank_i = 0
            self.tbanks = [(st.enter_context(nc.psum_tensor("tb%d" % i, [128, 1024], BF16)), Buf("tb%d" % i)) for i in range(2)]
            self.tbank_i = 0
            self.wsched = self.make_wsched()
            self.nblk_tile = len(self.wsched) // self.NT
            self.USE_WSCR = True
            self.wscr = nc.dram_tensor("wscr", [self.nblk_tile * 128, 4096], BF16, kind="Internal").ap()
            self.Bwscr = [Buf("wscr%d" % i) for i in range(self.nblk_tile)]
            self.w_cur = 0
            self.w_issued = 0
            self.out_toks = []

            ctoks = [P.dma("sp", self.cv[:, :], self.cvec_d, writes=[self.Bc], slot=4),
                     P.dma("sp", self.rows[:, :], self.rows_d.partition_broadcast(128), writes=[self.Bc], slot=5),
                     P.dma("sp", self.cm[:, :], self.cmat_d, writes=[self.Bc], slot=6),
                     P.dma("pool", self.cmb[:, :], self.cmatb_d, writes=[self.Bc], slot=7),
                     P.dma("pool", self.pw[:, :, :], self.poolw_d.rearrange("p (g d) -> p g d", g=4), writes=[self.Bc], slot=15)]
            for e_ in ("pe", "act", "dve", "pool"):
                P.wait_all(e_, ctoks)
            self.act(self.Arow[:, :], self.rows[:, R_ALOG:R_ALOG + 32], AF.Exp, [self.Bc], [self.BArow])
            self.ts("dve", self.Arow[:, :], self.Arow[:, :], -1.0, None, ALU.mult, None, [self.BArow], [self.BArow])
            self.act(self.esink[:, :], self.cv[:, C_SINK:C_SINK + 4], AF.Exp, [self.Bc], [self.Besink])
            P.op("dve", lambda e: e.memset(self.utail[:, :, :], 0.0), [], [self.Butail])
            P.op("dve", lambda e: e.memset(self.krT[:, :], 0.0), [], [self.Bkr])
            P.op("dve", lambda e: e.memset(self.vbuf[:, :, :], 0.0), [], [self.Bv])
            for l in range(2):
                P.op("pool", lambda e, l=l: e.memset(self.fcar[l][:, :, :, :], 0.0), [], self.Bfcar[l])
            P.op("pool", lambda e: e.memset(self.scar[:, :, :, :], 0.0), [], self.Bscar)
            P.op("pool", lambda e: e.memset(self.S[:, :], 0.0), [], self.BS)
            P.op("pool", lambda e: e.memset(self.Sbf[:, :], 0.0), [], self.BSbf)

            for t in range(self.NT):
                tok0 = t * TT
                if t == 0 or self.dbg_stage < 4:
                    for kt in range(8):
                        P.dma("sp", self.xT[:, kt, :], self.xT_d[kt * 128:(kt + 1) * 128, tok0:tok0 + TT], writes=[self.Bx[kt]], slot=28 + kt)
                nst = 0
                for fn in (lambda: self.stage_l0(t), lambda: self.stage_ffn(t, 0), lambda: self.stage_ssd(t), lambda: self.stage_ffn(t, 1)):
                    if nst < self.dbg_stage:
                        fn()
                    else:
                        pass
                    nst += 1
                if self.dbg_stage >= 4:
                    self.stage_final(t)
                else:
                    self.dbg_dump(t)
            P.wait_all("sp", self.out_toks)
            P.emit()
        return nc


def _col(v):
    v = np.asarray(v, np.float32)
    return np.ascontiguousarray(v.reshape(-1, 128).T)


def prep_shared(inp):
    f = lambda a: np.asarray(a, np.float32)
    cv = np.zeros((128, NCV), np.float32)
    cv[:, C_NM0:C_NM0 + 8] = _col(f(inp["norm_mix"])[0])
    cv[:, C_NF0:C_NF0 + 8] = _col(f(inp["norm_ffn"])[0])
    cv[:, C_NM1:C_NM1 + 8] = _col(f(inp["norm_mix"])[1])
    cv[:, C_NF1:C_NF1 + 8] = _col(f(inp["norm_ffn"])[1])
    cv[:, C_NFIN:C_NFIN + 8] = _col(f(inp["norm_final"]))
    cv[:, C_PSC:C_PSC + 4] = _col(f(inp["pool_scale"])[0])
    fcw = f(inp["ffn_conv_w"]); fcb = f(inp["ffn_conv_b"])
    for l in range(2):
        for k in range(3):
            c0 = C_FCW + (l * 3 + k) * 44
            cv[:, c0:c0 + 44] = _col(fcw[l, k])
        cv[:, C_FCB + l * 44:C_FCB + l * 44 + 44] = _col(fcb[l])
    scw = f(inp["ssm_conv_w"])[0]; scb = f(inp["ssm_conv_b"])[0]
    for k in range(4):
        cv[:, C_SCW + k * 32:C_SCW + k * 32 + 32] = _col(scw[k])
    cv[:, C_SCB:C_SCB + 32] = _col(scb)
    inv_freq = (np.float32(10000.0) ** (-np.arange(0, 64, 2, dtype=np.float32) / np.float32(64))).astype(np.float32)
    p = np.arange(128)
    cv[:, C_INVF] = inv_freq[p % 32]
    cv[:, C_SGN] = np.where((p % 64) < 32, -1.0, 1.0)
    sinks = f(inp["attn_sinks"])[0]
    for j in range(4):
        cv[:, C_SINK + j] = np.where(p < 64, sinks[j], sinks[4 + j])
    for g in range(4):
        w = 2 << g
        for tt_ in range(16):
            cv[:, C_ICNT + g * 16 + tt_] = 1.0 / min(tt_ + 1, w)
    cv[:, C_EPS] = 1e-6
    cv[:, C_EPS2] = 1e-5
    rows = np.zeros((1, NR), np.float32)
    rows[0, R_NW:R_NW + 2048] = f(inp["ssm_norm"])[0]
    rows[0, R_D:R_D + 32] = f(inp["ssm_D"])[0]
    rows[0, R_ALOG:R_ALOG + 32] = f(inp["ssm_A_log"])[0]
    rows[0, R_DTB:R_DTB + 32] = f(inp["ssm_dt_bias"])[0]
    i = np.arange(128)
    ident = np.eye(128, dtype=np.float32)
    tri = (i[:, None] <= i[None, :]).astype(np.float32)
    U = (i[None, :] < i[:, None]).astype(np.float32)
    cmat = np.concatenate([ident, tri, U, np.ones((128, 128), np.float32)], axis=1)
    mprev = np.where(i[:, None] > i[None, :], 0.0, NEG).astype(np.float32)
    mcur = np.where(i[:, None] <= i[None, :], 0.0, NEG).astype(np.float32)
    cmatb = np.concatenate([ident, np.tile(mprev, (1, 4)), np.tile(mcur, (1, 4)), np.ones((128, 128), np.float32)], axis=1)
    w_in = f(inp["mix_w_in"])[0]
    u_c = np.arange(0, 512)
    perm64 = np.concatenate([np.arange(32, 64), np.arange(0, 32)])
    qcols = []
    for j in range(4):
        hA = np.concatenate([512 + j * 64 + np.arange(64), 512 + (4 + j) * 64 + np.arange(64)])
        hP = np.concatenate([512 + j * 64 + perm64, 512 + (4 + j) * 64 + perm64])
        qcols += [hA, hP]
    kc = 1024 + np.arange(128)
    kp = np.concatenate([1024 + perm64, 1024 + 64 + perm64])
    vc = 1152 + np.arange(128)
    cols = np.concatenate([u_c] + qcols + [kc, kp, vc])
    win0 = np.ascontiguousarray(w_in[:, cols])
    w_out = f(inp["mix_w_out"])[0]
    rws = [np.arange(512)]
    for j in range(4):
        rws.append(np.concatenate([512 + j * 64 + np.arange(64), 512 + (4 + j) * 64 + np.arange(64)]))
    wout0 = np.ascontiguousarray(w_out[np.concatenate(rws), :])
    poolw = np.ascontiguousarray(f(inp["pool_w"])[0].transpose(1, 0, 2).reshape(128, 512))
    fup = []
    for l in range(2):
        wu = f(inp["ffn_w_up"])[l]
        wi = np.stack([wu[:, :DFF].reshape(D, 22, 128), wu[:, DFF:].reshape(D, 22, 128)], axis=2)
        fup.append(np.ascontiguousarray(wi.reshape(D, 2 * DFF)))
    def blockify(W, c0, nb, nk, ncol):
        Wb = W[:, c0:c0 + nb * ncol].reshape(nk, 128, nb, ncol).transpose(2, 1, 0, 3)
        return np.ascontiguousarray(Wb.reshape(nb * 128, nk * ncol))
    w_sin = f(inp["ssm_w_in"])[0]
    sh = {"cvec": cv, "rows": rows, "cmat": cmat, "cmatb": cmatb, "poolw": poolw,
          "win0a": blockify(win0, 0, 3, 8, 512), "win0b": blockify(win0, 1536, 1, 8, 384),
          "wout0": blockify(wout0, 0, 2, 8, 512),
          "fup0": blockify(fup[0], 0, 11, 8, 512), "fup1": blockify(fup[1], 0, 11, 8, 512),
          "fdn0": blockify(f(inp["ffn_w_down"])[0], 0, 8, 22, 128), "fdn1": blockify(f(inp["ffn_w_down"])[1], 0, 8, 22, 128),
          "sina": blockify(w_sin, 0, 12, 8, 512), "sinb": blockify(w_sin, 6144, 1, 8, 32),
          "sout": blockify(f(inp["ssm_w_out"])[0], 0, 8, 16, 128)}
    return sh


_CACHE = {}


def run(inp, S=SEQ, n_cores=4, dbg_stage=99):
    x = np.asarray(inp["x"], np.float32)
    pos = np.asarray(inp["positions"], np.int32)
    NT = S // TT
    key = (S, dbg_stage)
    if key not in _CACHE:
        _CACHE[key] = Builder(S, NT, dbg_stage).build()
    nc = _CACHE[key]
    sh = prep_shared(inp)
    in_maps = []
    for c in range(n_cores):
        b = c % x.shape[0]
        m = dict(sh)
        m["xT"] = np.ascontiguousarray(x[b, :S].T)
        m["pos"] = np.ascontiguousarray(pos[b:b + 1, :S])
        in_maps.append(m)
    res = run_bass_kernel_spmd(nc, in_maps, core_ids=list(range(n_cores)))
    outs = [np.ascontiguousarray(res.results[c]["outT"].T) for c in range(n_cores)]
    return outs


def kernel(**inputs):
    outs = run(inputs, SEQ, 4)
    return np.stack(outs[:4], axis=0).astype(np.float32)
```
